# Optimizing a Trainium2 kernel written in Bass

```python
import jax, jax.numpy as jnp
from jax import lax
import numpy as np

D_MODEL = 1024
BATCH = 8
SEQ = 2048
DEPTH = 1
DEC_BATCH = 128
DEC_SEQ = 1
PAST_LEN = 16384
PAGE_SIZE = 128

RET_HEADS = 4
RET_DK = D_MODEL // 8
RET_DV = D_MODEL // 8
RET_W = RET_HEADS * RET_DK
RET_VW = RET_HEADS * RET_DV
RET_CHUNK = 128
ROPE_BASE = 10000.0
CONV_W = D_MODEL // 2
CONV_K = 31
MEM_HEADS = 4
MEM_DH = D_MODEL // 8
MEM_W = MEM_HEADS * MEM_DH
N_MEM = 256
N_BRANCH = 3
IN_W = 2 * RET_W + 2 * RET_VW + 3 * CONV_W + 2 * MEM_W + N_BRANCH * D_MODEL
EPS = 1e-6

kernel_name = 'retnet_conformer_memxattn_gated_hybrid_step'

F32 = jnp.float32


def _rmsnorm(x, w):
    xf = x.astype(F32)
    y = xf * lax.rsqrt(jnp.mean(xf * xf, axis=-1, keepdims=True) + EPS)
    return (y * w.astype(F32)).astype(x.dtype)


def _layernorm(x, w, b):
    xf = x.astype(F32)
    mu = jnp.mean(xf, axis=-1, keepdims=True)
    xc = xf - mu
    var = jnp.mean(xc * xc, axis=-1, keepdims=True)
    y = xc * lax.rsqrt(var + EPS) * w.astype(F32) + b.astype(F32)
    return y.astype(x.dtype)


def _rotary(x, pos):
    half = x.shape[-1] // 2
    inv = ROPE_BASE ** (-jnp.arange(half, dtype=F32) / half)
    ang = pos[:, None] * inv[None, :]
    cos = jnp.cos(ang)[None, :, None, :]
    sin = jnp.sin(ang)[None, :, None, :]
    x1, x2 = x[..., :half], x[..., half:]
    return jnp.concatenate([x1 * cos - x2 * sin, x1 * sin + x2 * cos], axis=-1)


def _retention(q, k, v, s0):
    B, T, H, _ = q.shape
    DV = v.shape[-1]
    C = RET_CHUNK if T % RET_CHUNK == 0 else T
    n = T // C
    log_g = jnp.log1p(-jnp.exp2(-5.0 - jnp.arange(H, dtype=F32)))
    idx = jnp.arange(C, dtype=F32)
    diff = idx[:, None] - idx[None, :]
    intra = jnp.where(diff >= 0,
                      jnp.exp(jnp.maximum(diff, 0.0)[None] * log_g[:, None, None]),
                      0.0)
    q_dec = jnp.exp((idx + 1.0)[None, :] * log_g[:, None])
    k_dec = jnp.exp((C - 1.0 - idx)[None, :] * log_g[:, None])
    c_dec = jnp.exp(C * log_g)

    def to_chunks(a):
        return a.reshape(B, n, C, H, a.shape[-1]).transpose(1, 0, 3, 2, 4)

    def step(s, blk):
        qc, kc, vc = blk
        sc = jnp.einsum('bhid,bhjd->bhij', qc, kc) * intra[None]
        o = (jnp.einsum('bhij,bhje->bhie', sc, vc)
             + jnp.einsum('bhid,bhde->bhie', qc, s) * q_dec[None, :, :, None])
        s = (s * c_dec[None, :, None, None]
             + jnp.einsum('bhjd,bhje->bhde', kc * k_dec[None, :, :, None], vc))
        return s, o

    s_fin, o = lax.scan(step, s0.astype(F32), (to_chunks(q), to_chunks(k), to_chunks(v)))
    o = o.transpose(1, 0, 3, 2, 4).reshape(B, T, H, DV)
    return o, s_fin


def _group_norm_heads(o, w):
    B, T, H, DV = o.shape
    mu = jnp.mean(o, axis=-1, keepdims=True)
    oc = o - mu
    var = jnp.mean(oc * oc, axis=-1, keepdims=True)
    y = (oc * lax.rsqrt(var + EPS)).reshape(B, T, H * DV)
    return y * w.astype(F32)


def _causal_dwconv(u, buf, w, b):
    full = jnp.concatenate([buf.astype(u.dtype), u], axis=1)
    y = lax.conv_general_dilated(
        full, w.astype(u.dtype)[:, None, :], window_strides=(1,), padding='VALID',
        dimension_numbers=('NWC', 'WIO', 'NWC'), feature_group_count=u.shape[-1])
    return y + b.astype(u.dtype), full[:, -(CONV_K - 1):]


def _mem_kv(mem, mem_norm_w, w_mem_kv):
    B = mem.shape[0]
    kv = _rmsnorm(mem, mem_norm_w) @ w_mem_kv
    k, v = jnp.split(kv, 2, axis=-1)
    return (k.reshape(B, -1, MEM_HEADS, MEM_DH), v.reshape(B, -1, MEM_HEADS, MEM_DH))


def _mem_attend(q, k, v):
    B, T, _ = q.shape
    qh = q.reshape(B, T, MEM_HEADS, MEM_DH).astype(F32)
    s = jnp.einsum('bthd,bmhd->bhtm', qh, k.astype(F32)) * (MEM_DH ** -0.5)
    p = jax.nn.softmax(s, axis=-1)
    o = jnp.einsum('bhtm,bmhd->bthd', p, v.astype(F32))
    return o.reshape(B, T, MEM_W).astype(q.dtype)


def _layer(x, pos0, s0, conv_buf, mem_k, mem_v, norm_w, w_in, ret_gn_w, conv_w, conv_b,
           conv_ln_w, conv_ln_b, w_br_ret, w_br_conv, w_br_mem, w_out):
    B, T, _ = x.shape
    h = _rmsnorm(x, norm_w)
    z = h @ w_in
    sizes = (RET_W, RET_W, RET_VW, RET_VW, CONV_W, CONV_W, CONV_W, MEM_W, MEM_W)
    offs = np.cumsum(sizes).tolist()
    q_r, k_r, v_r, g_r, a_c, b_c, g_c, q_m, g_m, g_merge = jnp.split(z, offs, axis=-1)

    pos = jnp.arange(T, dtype=F32) + pos0
    q = _rotary(q_r.reshape(B, T, RET_HEADS, RET_DK).astype(F32), pos) * (RET_DK ** -0.5)
    k = _rotary(k_r.reshape(B, T, RET_HEADS, RET_DK).astype(F32), pos)
    v = v_r.reshape(B, T, RET_HEADS, RET_DV).astype(F32)
    o, s_new = _retention(q, k, v, s0)
    ret = _group_norm_heads(o, ret_gn_w).astype(x.dtype)
    br_r = (jax.nn.silu(g_r) * ret) @ w_br_ret

    u = a_c * jax.nn.sigmoid(b_c)
    c, buf_new = _causal_dwconv(u, conv_buf, conv_w, conv_b)
    c = jax.nn.silu(_layernorm(c, conv_ln_w, conv_ln_b))
    br_c = (jax.nn.silu(g_c) * c) @ w_br_conv

    m = _mem_attend(q_m, mem_k, mem_v)
    br_m = (jax.nn.silu(g_m) * m) @ w_br_mem

    g = jax.nn.sigmoid(g_merge).reshape(B, T, N_BRANCH, D_MODEL)
    merged = g[:, :, 0] * br_r + g[:, :, 1] * br_c + g[:, :, 2] * br_m
    return x + merged @ w_out, s_new, buf_new


def setup_inputs(seed: int = 0) -> dict:
    key = jax.random.key(seed)
    ks = jax.random.split(key, 21)

    def nrm(k, shape, s):
        return s * jax.random.normal(k, shape, F32)

    return {
        'x_prompt': nrm(ks[0], (BATCH, SEQ, D_MODEL), 1.0),
        'x_sample': nrm(ks[1], (DEC_BATCH, DEC_SEQ, D_MODEL), 1.0),
        'mem_prompt': nrm(ks[2], (BATCH, N_MEM, D_MODEL), 1.0),
        'state_ret': nrm(ks[3], (DEPTH, DEC_BATCH, RET_HEADS, RET_DK, RET_DV), 4.0),
        'state_conv': nrm(ks[4], (DEPTH, DEC_BATCH, CONV_K - 1, CONV_W), 0.5),
        'cache_mem_k': nrm(ks[5], (DEPTH, DEC_BATCH, N_MEM, MEM_HEADS, MEM_DH), 1.0),
        'cache_mem_v': nrm(ks[6], (DEPTH, DEC_BATCH, N_MEM, MEM_HEADS, MEM_DH), 1.0),
        'norm_w': 1.0 + nrm(ks[7], (DEPTH, D_MODEL), 0.02),
        'w_in': nrm(ks[8], (DEPTH, D_MODEL, IN_W), D_MODEL ** -0.5),
        'ret_gn_w': 1.0 + nrm(ks[9], (DEPTH, RET_VW), 0.02),
        'conv_w': nrm(ks[10], (DEPTH, CONV_K, CONV_W), CONV_K ** -0.5),
        'conv_b': nrm(ks[11], (DEPTH, CONV_W), 0.02),
        'conv_ln_w': 1.0 + nrm(ks[12], (DEPTH, CONV_W), 0.02),
        'conv_ln_b': nrm(ks[13], (DEPTH, CONV_W), 0.02),
        'mem_norm_w': 1.0 + nrm(ks[14], (DEPTH, D_MODEL), 0.02),
        'w_mem_kv': nrm(ks[15], (DEPTH, D_MODEL, 2 * MEM_W), D_MODEL ** -0.5),
        'w_br_ret': nrm(ks[16], (DEPTH, RET_VW, D_MODEL), RET_VW ** -0.5),
        'w_br_conv': nrm(ks[17], (DEPTH, CONV_W, D_MODEL), CONV_W ** -0.5),
        'w_br_mem': nrm(ks[18], (DEPTH, MEM_W, D_MODEL), MEM_W ** -0.5),
        'w_out': nrm(ks[19], (DEPTH, D_MODEL, D_MODEL), D_MODEL ** -0.5),
        'final_norm_w': 1.0 + nrm(ks[20], (D_MODEL,), 0.02),
    }


def reference(x_prompt, x_sample, mem_prompt, state_ret, state_conv, cache_mem_k, cache_mem_v,
              norm_w, w_in, ret_gn_w, conv_w, conv_b, conv_ln_w, conv_ln_b, mem_norm_w,
              w_mem_kv, w_br_ret, w_br_conv, w_br_mem, w_out, final_norm_w):
    hp, hs = x_prompt, x_sample
    bp = x_prompt.shape[0]
    ret_p, ret_s, conv_p, conv_s, mk_p, mv_p = [], [], [], [], [], []
    for l in range(DEPTH):
        lw = (norm_w[l], w_in[l], ret_gn_w[l], conv_w[l], conv_b[l], conv_ln_w[l],
              conv_ln_b[l], w_br_ret[l], w_br_conv[l], w_br_mem[l], w_out[l])
        mk, mv = _mem_kv(mem_prompt, mem_norm_w[l], w_mem_kv[l])
        s0 = jnp.zeros((bp, RET_HEADS, RET_DK, RET_DV), F32)
        buf0 = jnp.zeros((bp, CONV_K - 1, CONV_W), hp.dtype)
        hp, sp, cp = _layer(hp, 0, s0, buf0, mk, mv, *lw)
        hs, ss, cs = _layer(hs, PAST_LEN, state_ret[l], state_conv[l],
                            cache_mem_k[l], cache_mem_v[l], *lw)
        ret_p.append(sp)
        ret_s.append(ss)
        conv_p.append(cp)
        conv_s.append(cs)
        mk_p.append(mk)
        mv_p.append(mv)
    y_prompt = _rmsnorm(hp, final_norm_w)
    y_sample = _rmsnorm(hs, final_norm_w)
    return (y_prompt, y_sample, jnp.stack(ret_p), jnp.stack(ret_s), jnp.stack(conv_p),
            jnp.stack(conv_s), jnp.stack(mk_p), jnp.stack(mv_p))
```

```python
import numpy as np
import concourse.bass as bass
import concourse.mybir as mybir
from concourse.bass_utils import run_bass_kernel_spmd

F32 = mybir.dt.float32
BF16 = mybir.dt.bfloat16
ALU = mybir.AluOpType
AF = mybir.ActivationFunctionType
AX = mybir.AxisListType

ENGS = ['tensor', 'vector', 'scalar', 'gpsimd', 'sync']
EPS = 1e-6
NTOK = 2064


class Res:
    __slots__ = ('w', 'r')

    def __init__(self):
        self.w = None
        self.r = []


class DmaSem:
    def __init__(self, sem):
        self.sem = sem
        self.count = 0


class Prog:
    def __init__(self, nc):
        self.nc = nc
        self.sem, self.n, self.ops, self.waited = {}, {}, {}, {}
        self.semobjs = {}
        for e in ENGS:
            self.sem[e] = nc.alloc_semaphore(name=f"prog_{e}")
            self.n[e] = 0
            self.ops[e] = []
            self.waited[e] = {}
            self.semobjs[id(self.sem[e])] = self.sem[e]
        self.dsems = []

    def dma_sem(self):
        s = self.nc.alloc_semaphore(name=f"dsem{len(self.dsems)}")
        self.semobjs[id(s)] = s
        d = DmaSem(s)
        self.dsems.append(d)
        return d

    def _wait(self, eng, ev):
        semid, val = ev
        if self.waited[eng].get(semid, 0) >= val:
            return
        self.waited[eng][semid] = val
        self.ops[eng].append(('wait', self.semobjs[semid], val))

    def _deps(self, eng, reads, writes, own2=None):
        own = id(self.sem[eng])
        for r in reads:
            if r.w is not None:
                self._wait(eng, r.w)
        skip_own = (eng == 'tensor')
        for w in writes:
            if w.w is not None and w.w[0] != own2 and not (skip_own and w.w[0] == own):
                self._wait(eng, w.w)
            for ev in w.r:
                if not (skip_own and ev[0] == own):
                    self._wait(eng, ev)

    def op(self, eng, fn, reads=(), writes=(), inc=True):
        self._deps(eng, reads, writes)
        ev = (id(self.sem[eng]), self.n[eng] + 1)
        if inc:
            self.n[eng] += 1
            self.ops[eng].append(('op', fn, self.sem[eng], 1))
        else:
            self.ops[eng].append(('op', fn, None, 0))
        for r in reads:
            r.r.append(ev)
        for w in writes:
            w.w = ev
            w.r = []
        return ev

    def dma(self, eng, dsem, fn, reads=(), writes=()):
        self._deps(eng, reads, writes, own2=id(dsem.sem))
        dsem.count += 16
        ev = (id(dsem.sem), dsem.count)
        self.ops[eng].append(('op', fn, dsem.sem, 16))
        for r in reads:
            r.r.append(ev)
        for w in writes:
            w.w = ev
            w.r = []
        return ev

    def barrier(self, engines=('vector', 'scalar', 'gpsimd', 'sync')):
        for e in engines:
            for o in ENGS:
                if o != e and self.n[o] > 0:
                    self._wait(e, (id(self.sem[o]), self.n[o]))
            for d in self.dsems:
                if d.count > 0:
                    self._wait(e, (id(d.sem), d.count))

    def replay(self):
        with self.nc.Block() as block:
            for e in ENGS:
                ops = self.ops[e]

                def body(engobj, ops=ops):
                    for item in ops:
                        if item[0] == 'wait':
                            engobj.wait_ge(item[1], item[2])
                        else:
                            ins = item[1](engobj)
                            if item[2] is not None:
                                ins.then_inc(item[2], item[3])
                getattr(block, e)(body)


class Arena:
    def __init__(self, ap, nwords):
        self.ap = ap
        self.n = nwords
        self.off = 0

    def f32(self, n):
        assert self.off + n <= self.n, ("arena overflow", self.off, n, self.n)
        a = self.ap[:, self.off:self.off + n]
        self.off += n
        return a

    def bf16(self, n):
        assert n % 2 == 0
        return self.f32(n // 2).bitcast(BF16)


def make_consts():
    f32 = np.float32
    c = {}
    c['identf'] = np.eye(128, dtype=f32)
    j = np.arange(128)
    c['cmask'] = (j[None, :] >= j[:, None]).astype(f32)
    inv = (10000.0 ** (-np.arange(64, dtype=f32) / f32(64))).astype(f32)

    def rot(pos):
        ang = (pos.astype(f32)[:, None] * inv[None, :]).astype(f32).astype(np.float64)
        cs = np.cos(ang).astype(f32)
        sn = np.sin(ang).astype(f32)
        return cs, np.stack([-sn, sn], axis=1)

    cs, sn2 = rot(np.arange(2048))
    c['cosP'] = np.ascontiguousarray(cs.reshape(16, 128, 64).transpose(1, 0, 2))
    c['sinP'] = np.ascontiguousarray(sn2.reshape(16, 128, 2, 64).transpose(1, 0, 2, 3))
    cs, sn2 = rot(np.full((16,), 16384.0))
    c['cosS'] = cs
    c['sinS'] = sn2
    g = (1.0 - 2.0 ** (-5.0 - np.arange(4, dtype=np.float64)))
    jj = np.arange(128, dtype=np.float64)
    c['gk'] = (g[None, :] ** (-(jj[:, None] + 1.0))).astype(f32)
    c['epsT'] = (EPS * 128.0 / (g[None, :] ** (2.0 * (jj[:, None] + 1.0)))).astype(f32)
    c['ctab'] = np.repeat((g ** 128.0)[None, :], 128, axis=0).repeat(128, axis=1).astype(f32)
    c['gtab16'] = np.repeat(g[None, :], 16, axis=0).repeat(128, axis=1).astype(f32)
    gI = np.zeros((128, 4, 128), f32)
    for h in range(4):
        gI[:, h, :] = np.eye(128) * g[h]
    c['gI'] = gI
    c['delta16'] = np.ascontiguousarray(np.broadcast_to(np.eye(16, dtype=f32)[None], (128, 16, 16)))
    ind = np.zeros((120, 4, 16), f32)
    for i in range(4):
        for bb in range(4):
            ind[bb * 30:(bb + 1) * 30, i, 4 * i + bb] = 1.0
    c['ind'] = ind
    sel = np.zeros((16, 16, 128), f32)
    for b in range(16):
        sel[b, b, :] = 1.0
    c['sel'] = sel
    return c


CONST_SHAPES = {
    'identf': [128, 128], 'cmask': [128, 128], 'cosP': [128, 16, 64], 'sinP': [128, 16, 2, 64],
    'cosS': [16, 64], 'sinS': [16, 2, 64], 'gk': [128, 4], 'epsT': [128, 4], 'ctab': [128, 512],
    'gtab16': [16, 512], 'gI': [128, 4, 128], 'delta16': [128, 16, 16], 'ind': [120, 4, 16],
    'sel': [16, 16, 128],
}

IN_SHAPES = {
    'xp': [2048, 1024], 'xs': [16, 1024], 'memp': [256, 1024], 'sret': [16, 4, 128, 128],
    'sconv': [16, 30, 512], 'ck': [16, 256, 512], 'cv': [16, 256, 512],
    'norm_w': [1024], 'w_in': [1024, 7680], 'gnwT': [128, 4], 'cwT': [128, 4, 31], 'wrep': [120, 512],
    'w30b': [16, 512], 'cbb': [16, 512], 'cbT': [128, 4], 'lnwT': [128, 4], 'lnbT': [128, 4],
    'lnwb': [16, 512], 'lnbb': [16, 512], 'mem_norm_w': [1024], 'w_mem_kv': [1024, 1024],
    'w_br_ret': [512, 1024], 'w_br_conv': [512, 1024], 'w_br_mem': [512, 1024], 'w_out': [1024, 1024],
    'final_norm_w': [1024],
}
OUT_SHAPES = {
    'yp': [2048, 1024], 'ys': [16, 1024], 'srp': [4, 128, 128], 'srs': [16, 4, 128, 128],
    'scp': [30, 512], 'scs': [16, 30, 512], 'mkp': [256, 512], 'mvp': [256, 512],
}


def build_program():
    nc = bass.Bass("TRN2", target_bir_lowering=False)
    D = {}
    for k, s in IN_SHAPES.items():
        D[k] = nc.dram_tensor(k, s, F32, kind="ExternalInput").ap()
    for k, s in CONST_SHAPES.items():
        D[k] = nc.dram_tensor("c_" + k, s, F32, kind="ExternalInput").ap()
    for k, s in OUT_SHAPES.items():
        D[k] = nc.dram_tensor(k, s, F32, kind="ExternalOutput").ap()

    P = Prog(nc)
    NW = 52992
    arena_ap = nc.alloc_sbuf_tensor("arena", [128, NW], F32).ap()
    A = Arena(arena_ap, NW)
    banks = [nc.alloc_psum_tensor(f"bank{i}", [128, 512], F32).ap() for i in range(8)]
    bankb = [b.bitcast(BF16) for b in banks]
    r_bank = [Res() for _ in range(8)]

    def op(eng, fn, reads=(), writes=(), inc=True):
        return P.op(eng, fn, reads, writes, inc)

    def mm(out, lhsT, rhs, start, stop, reads, writes, inc):
        P.op('tensor', lambda e: e.matmul(out, lhsT=lhsT, rhs=rhs, start=start, stop=stop), reads, writes, inc)

    def tr(out, in_, ident, reads, writes, inc):
        P.op('tensor', lambda e: e.transpose(out=out, in_=in_, identity=ident), reads, writes, inc)

    out_sem = P.dma_sem()
    ld_sem_n = [0]

    class LoadGroup:
        def __init__(self):
            self.d = P.dma_sem()
            self.res = []

        def add(self, eng, out, in_, res):
            P.dma(eng, self.d, lambda e: e.dma_start(out=out, in_=in_), reads=[], writes=[res])
            self.res.append(res)

        def done(self):
            ev = (id(self.d.sem), self.d.count)
            for r in self.res:
                r.w = ev

    cur_grp = [None]

    def load(eng, out, in_, res):
        if cur_grp[0] is None:
            cur_grp[0] = LoadGroup()
        cur_grp[0].add(eng, out, in_, res)

    def loads_done():
        if cur_grp[0] is not None:
            cur_grp[0].done()
            cur_grp[0] = None

    def store(eng, out, in_, reads, dsem=None):
        P.dma(eng, out_sem if dsem is None else dsem, lambda e: e.dma_start(out=out, in_=in_), reads=reads)

    hT = A.bf16(8 * NTOK).rearrange("p (k t) -> p k t", k=8)
    r_hT = [Res() for _ in range(17)]
    preT = [A.bf16(4 * NTOK).rearrange("p (k t) -> p k t", k=4) for _ in range(3)]
    r_preT = [[Res() for _ in range(5)] for _ in range(3)]
    WA = Arena(A.f32(8192), 8192)
    WB = Arena(A.f32(8192), 8192)
    identf = A.f32(128)
    identb = A.bf16(128)
    ones_bf = A.bf16(128)
    onesS_bf = A.bf16(128)
    ones_f = A.f32(1)
    mhalf = A.f32(512)
    kT = A.bf16(4 * 256).rearrange("p (h m) -> p h m", h=4)
    vmem = A.bf16(2 * 512).rearrange("p (c f) -> p c f", c=2)
    gnwT = A.f32(4)
    r_const = Res()
    r_kv = Res()
    PERS = A.off

    load('sync', identf, D['identf'], r_const)
    load('sync', gnwT, D['gnwT'], r_const)
    loads_done()
    op('vector', lambda e: e.tensor_copy(out=identb, in_=identf), [r_const], [r_const])
    op('vector', lambda e: e.memset(ones_bf, 1.0), [], [r_const])
    op('vector', lambda e: e.memset(onesS_bf, 1.0 / 512.0), [], [r_const])
    op('vector', lambda e: e.memset(ones_f, 1.0), [], [r_const])
    op('vector', lambda e: e.memset(mhalf, -0.5), [], [r_const])

    def rsqrt_inplace(ap, res):
        p, n = ap.shape[0], ap.shape[1]
        P.op('gpsimd', lambda e: e.tensor_tensor(out=ap, in0=ap, in1=mhalf[0:p, 0:n], op=ALU.pow), [res, r_const], [res])

    w_in_v = D['w_in'].rearrange("(k p) n -> p k n", p=128)

    def load_w(dst, src, res, nsplit=1):
        n = dst.shape[1]
        step = n // nsplit
        g = LoadGroup()
        for i in range(nsplit):
            sl = slice(i * step, (i + 1) * step)
            g.add('gpsimd', dst[:, sl, :], src[:, sl, :], res)
        g.done()

    def phase_A(wkv, r_wkv, load_wB, load_wkv):
        NXS = 6
        xst = [A.f32(1024) for _ in range(NXS)]
        r_xst = [Res() for _ in range(NXS)]
        d_xst = [P.dma_sem() for _ in range(NXS)]
        xsb = [A.bf16(1024) for _ in range(2)]
        r_xsb = [Res() for _ in range(2)]
        junk = A.bf16(1024)
        r_junk = Res()
        nwb = A.f32(1024)
        mnwb = A.f32(1024)
        ss = A.f32(32)
        rstd = A.f32(32)
        memhT = A.bf16(8 * 256).rearrange("p (k t) -> p k t", k=8)
        kvout = [A.f32(512) for _ in range(2)]
        r_kvout = [Res() for _ in range(2)]
        d_kvout = [P.dma_sem() for _ in range(2)]
        r_nw, r_memhT = Res(), Res()
        r_ss = [Res() for _ in range(32)]
        load('sync', nwb, D['norm_w'].partition_broadcast(128), r_nw)
        load('sync', mnwb, D['mem_norm_w'].partition_broadcast(128), r_nw)
        loads_done()

        tiles = []
        for t in range(2):
            tiles.append((D['memp'][t * 128:(t + 1) * 128, :], 128, mnwb, 'mem', t))
        for t in range(16):
            tiles.append((D['xp'][t * 128:(t + 1) * 128, :], 128, nwb, 'x', t))
        tiles.append((D['xs'], 16, nwb, 'x', 16))

        def s1(i):
            src, nr, nt, kind, t = tiles[i]
            s = i % NXS
            xt = xst[s][0:nr]
            ssi, rsi = ss[0:nr, i:i + 1], rstd[0:nr, i:i + 1]
            op('scalar', lambda e: e.activation(out=junk[0:nr], in_=xt, func=AF.Square, accum_out=ssi), [r_xst[s]], [r_ss[i], r_junk])
            op('gpsimd', lambda e: e.tensor_scalar(out=rsi, in0=ssi, scalar1=1.0 / 1024, scalar2=EPS, op0=ALU.mult, op1=ALU.add),
               [r_ss[i]], [r_ss[i]])
            rsqrt_inplace(rsi, r_ss[i])

        def s2(i):
            src, nr, nt, kind, t = tiles[i]
            s, b = i % NXS, i % 2
            xt = xst[s][0:nr]
            rsi = rstd[0:nr, i:i + 1]
            xb = xsb[b][0:nr]
            op('vector', lambda e: e.scalar_tensor_tensor(out=xb, in0=xt, scalar=rsi, in1=nt[0:nr], op0=ALU.mult, op1=ALU.mult),
               [r_xst[s], r_ss[i], r_nw], [r_xsb[b]])
            bk = i % 2
            for kc in range(8):
                tr(bankb[bk][:, kc * nr:(kc + 1) * nr], xb[:, kc * 128:(kc + 1) * 128], identb[0:nr, 0:nr],
                   [r_xsb[b], r_const], [r_bank[bk]], kc == 7)
            src_v = bankb[bk][:, 0:8 * nr].rearrange("p (k t) -> p k t", k=8)
            if kind == 'mem':
                dst, rr = memhT[:, :, t * 128:(t + 1) * 128], r_memhT
            elif t < 16:
                dst, rr = hT[:, :, t * 128:(t + 1) * 128], r_hT[t]
            else:
                dst, rr = hT[:, :, 2048:2064], r_hT[16]
            op('scalar', lambda e: e.copy(out=dst, in_=src_v), [r_bank[bk]], [rr])

        def memkv():
            for mt in range(2):
                for half in range(2):
                    bk2 = 2 + (mt * 2 + half) % 2
                    for kc in range(8):
                        mm(banks[bk2], memhT[:, kc, mt * 128:(mt + 1) * 128], wkv[:, kc, half * 512:(half + 1) * 512],
                           kc == 0, kc == 7, [r_memhT, r_wkv], [r_bank[bk2]], kc == 7)
                    ko = kvout[half]
                    op('scalar', lambda e, ko=ko, bk2=bk2: e.copy(out=ko, in_=banks[bk2]), [r_bank[bk2]], [r_kvout[half]])
                    if half == 1:
                        op('scalar', lambda e, mt=mt, bk2=bk2: e.activation(out=vmem[:, mt, :], in_=banks[bk2], func=AF.Copy),
                           [r_bank[bk2]], [r_kv])
                    dst_d = (D['mkp'] if half == 0 else D['mvp'])[mt * 128:(mt + 1) * 128, :]
                    store('gpsimd', dst_d, ko, [r_kvout[half]], d_kvout[half])
            for h in range(4):
                bk2 = 4 + h % 2
                for kc in range(8):
                    mm(banks[bk2][:, 0:256], wkv[:, kc, h * 128:(h + 1) * 128], memhT[:, kc, :],
                       kc == 0, kc == 7, [r_memhT, r_wkv], [r_bank[bk2]], kc == 7)
                op('scalar', lambda e, h=h, bk2=bk2: e.mul(out=kT[:, h, :], in_=banks[bk2][:, 0:256], mul=128.0 ** -0.5),
                   [r_bank[bk2]], [r_kv])

        def ld(i):
            src, nr, nt, kind, t = tiles[i]
            s = i % NXS
            xt = xst[s][0:nr]
            P.dma('sync', d_xst[s], lambda e: e.dma_start(out=xt, in_=src), [], [r_xst[s]])

        n = len(tiles)
        for i in range(min(NXS - 1, n)):
            ld(i)
        s1(0)
        s1(1)
        P._deps('gpsimd', [r_xst[2]], [])
        load_wkv()
        for i in range(n):
            if i + NXS - 1 < n:
                ld(i + NXS - 1)
            if i + 2 < n:
                s1(i + 2)
            s2(i)
            if i == 10:
                memkv()
            if i == 10:
                load_wB()

    def phase_B(wBv, r_wB, CV, late_loads):
        diag, r_diag, cwT, cwTh, cbT, lnwT, lnbT, r_cv = CV
        cosP = A.f32(16 * 64).rearrange("p (t f) -> p t f", t=16)
        sinP = A.f32(16 * 128).rearrange("p (t a f) -> p t a f", t=16, a=2)
        gk, epsT, cmask, ctab = A.f32(4), A.f32(4), A.f32(128), A.f32(512)
        r_tab = Res()
        r_rot = [Res() for _ in range(16)]
        load('sync', gk, D['gk'], r_tab)
        load('sync', epsT, D['epsT'], r_tab)
        load('sync', cmask, D['cmask'], r_tab)
        load('sync', ctab, D['ctab'], r_tab)
        load('sync', cosP[:, 0:2, :], D['cosP'][:, 0:2, :], r_rot[0])
        load('sync', sinP[:, 0:2, :, :], D['sinP'][:, 0:2, :, :], r_rot[1])
        loads_done()
        r_rot[0].w = r_rot[1].w
        load('sync', cosP[:, 2:16, :], D['cosP'][:, 2:16, :], r_rot[2])
        load('sync', sinP[:, 2:16, :, :], D['sinP'][:, 2:16, :, :], r_rot[3])
        load('sync', cwT, D['cwT'], r_cv)
        load('sync', cbT, D['cbT'], r_cv)
        load('sync', lnwT, D['lnwT'], r_cv)
        load('sync', lnbT, D['lnbT'], r_cv)
        load('sync', cosS_p[0:16], D['cosS'], r_rotS)
        load('sync', sinS_p[0:16], D['sinS'], r_rotS)
        loads_done()
        for i in range(4, 16):
            r_rot[i] = r_rot[3]
        r_rot[2].w = r_rot[3].w
        zq_sb, Bq, zk_sb, Bk = A.f32(512), A.f32(512), A.f32(512), A.f32(512)
        r_zq, r_zk = Res(), Res()
        r_Bq, r_Bk = [Res(), Res()], [Res(), Res()]
        qb = [A.bf16(512) for _ in range(2)]
        kb = [A.bf16(512) for _ in range(2)]
        vtb = [A.bf16(512) for _ in range(3)]
        r_vtb3 = [Res() for _ in range(3)]
        sg = [A.bf16(512) for _ in range(4)]
        r_sg3 = [Res() for _ in range(4)]
        nm4 = A.f32(4)
        negm4 = A.f32(4)
        r_nm = Res()
        qkT = [A.bf16(1024) for _ in range(2)]
        scT = [A.bf16(512) for _ in range(2)]
        prer = [A.bf16(512) for _ in range(2)]
        r_qb, r_kb, r_vtb, r_sg, r_qkT, r_scT, r_prer = [[Res(), Res()] for _ in range(7)]
        retn = A.f32(512)
        r_retn = [Res() for _ in range(4)]
        s_f, tmp = A.f32(512), A.f32(512)
        s_bf = A.bf16(512)
        r_s, r_tmp, r_sbf = Res(), Res(), Res()
        bnst, mv, rs4 = A.f32(24), A.f32(8), A.f32(4)
        r_st = Res()
        r_bn4 = [Res() for _ in range(4)]
        r_mv4 = [Res() for _ in range(4)]
        op('vector', lambda e: e.memset(s_f, 0.0), [], [r_s])
        op('vector', lambda e: e.memset(s_bf, 0.0), [], [r_sbf])
        ZQ, ZK, ZV, ZG, PT, SK, OO0, OO1 = range(8)
        OOB = [OO0, OO1]
        SC = KS = SK
        OO = OO0

        def v4(ap):
            return ap.rearrange("p (h a f) -> p h a f", h=4, a=2)

        def rotary(zsb, r_z, Bx, r_B, t, outb, r_out):
            zb = v4(zsb)
            for a in range(2):
                sn = sinP[:, t, a, :].unsqueeze(1).to_broadcast([128, 4, 64])
                op('vector', lambda e, a=a, sn=sn: e.tensor_tensor(out=v4(Bx)[:, :, a, :], in0=zb[:, :, 1 - a, :], in1=sn, op=ALU.mult),
                   [r_z, r_rot[t]], [r_B[a]])
            cs = cosP[:, t, :].unsqueeze(1).unsqueeze(1).to_broadcast([128, 4, 2, 64])
            op('vector', lambda e: e.tensor_tensor(out=zb, in0=zb, in1=cs, op=ALU.mult), [r_z, r_rot[t]] + r_B, [r_z])
            op('gpsimd', lambda e: e.tensor_tensor(out=outb, in0=zsb, in1=Bx, op=ALU.add), [r_z] + r_B, [r_out])

        ZBANK = {3: ZG, 2: ZV, 1: ZK, 0: ZQ}

        def z_blk(t, blk):
            bank = ZBANK[blk]
            for kc in range(8):
                mm(banks[bank], hT[:, kc, t * 128:(t + 1) * 128], wBv[:, kc, blk * 512:(blk + 1) * 512],
                   kc == 0, kc == 7, [r_hT[t], r_wB], [r_bank[bank]], kc == 7)

        def ev_g(t):
            op('scalar', lambda e: e.activation(out=sg[t % 4], in_=banks[ZG], func=AF.Silu), [r_bank[ZG]], [r_sg3[t % 4]])

        def ev_v(t):
            gkb = gk.unsqueeze(2).to_broadcast([128, 4, 128])
            op('vector', lambda e: e.tensor_tensor(out=vtb[t % 3].rearrange("p (h f) -> p h f", h=4),
                                                   in0=banks[ZV].rearrange("p (h f) -> p h f", h=4), in1=gkb, op=ALU.mult),
               [r_bank[ZV], r_tab], [r_vtb3[t % 3]])

        def ev_kq(t):
            op('scalar', lambda e: e.copy(out=zk_sb, in_=banks[ZK]), [r_bank[ZK]], [r_zk])
            op('scalar', lambda e: e.copy(out=zq_sb, in_=banks[ZQ]), [r_bank[ZQ]], [r_zq])

        def rot_k(t):
            rotary(zk_sb, r_zk, Bk, r_Bk, t, kb[t % 2], r_kb[t % 2])

        def rot_q(t):
            rotary(zq_sb, r_zq, Bq, r_Bq, t, qb[t % 2], r_qb[t % 2])

        def tail1a(t):
            b = t % 2
            for h in range(4):
                tr(bankb[PT][:, h * 128:(h + 1) * 128], qb[b][:, h * 128:(h + 1) * 128], identb, [r_qb[b], r_const], [r_bank[PT]], False)
            for h in range(4):
                tr(bankb[PT][:, (4 + h) * 128:(5 + h) * 128], kb[b][:, h * 128:(h + 1) * 128], identb, [r_kb[b], r_const], [r_bank[PT]], h == 3)
            op('scalar', lambda e: e.copy(out=qkT[b], in_=bankb[PT]), [r_bank[PT]], [r_qkT[b]])

        def tail1b_sc(t):
            b = t % 2
            for h in range(4):
                mm(banks[SK][:, h * 128:(h + 1) * 128], qkT[b][:, (4 + h) * 128:(5 + h) * 128], qkT[b][:, h * 128:(h + 1) * 128],
                   True, True, [r_qkT[b]], [r_bank[SK]], h == 3)
            cmb = cmask.unsqueeze(1).to_broadcast([128, 4, 128])
            op('vector', lambda e: e.tensor_tensor(out=scT[b].rearrange("p (h f) -> p h f", h=4),
                                                   in0=banks[SK].rearrange("p (h f) -> p h f", h=4), in1=cmb, op=ALU.mult),
               [r_bank[SK], r_tab], [r_scT[b]])

        def tail1b_os(t):
            b = t % 2
            v3 = t % 3
            ob = OOB[t % 2]
            for h in range(4):
                sl = slice(h * 128, (h + 1) * 128)
                mm(banks[ob][:, sl], scT[b][:, sl], vtb[v3][:, sl], True, False, [r_scT[b], r_vtb3[v3]], [r_bank[ob]], False)
                mm(banks[ob][:, sl], qkT[b][:, sl], s_bf[:, sl], False, True, [r_qkT[b], r_sbf], [r_bank[ob]], h == 3)
            for h in range(4):
                sl = slice(h * 128, (h + 1) * 128)
                mm(banks[SK][:, sl], kb[b][:, sl], vtb[v3][:, sl], True, True, [r_kb[b], r_vtb3[v3]], [r_bank[SK]], h == 3)
            op('vector', lambda e: e.tensor_tensor(out=tmp, in0=banks[SK], in1=s_f, op=ALU.add), [r_bank[SK], r_s], [r_tmp])
            op('vector', lambda e: e.tensor_tensor(out=s_bf, in0=tmp, in1=ctab, op=ALU.mult), [r_tmp, r_tab], [r_sbf])
            op('gpsimd', lambda e: e.tensor_tensor(out=s_f, in0=tmp, in1=ctab, op=ALU.mult), [r_tmp, r_tab], [r_s])

        def tail2a(t):
            b = t % 2
            ob = OOB[t % 2]
            for h in range(4):
                op('vector', lambda e, h=h: e.bn_stats(out=bnst[:, h * 6:(h + 1) * 6], in_=banks[ob][:, h * 128:(h + 1) * 128]),
                   [r_bank[ob]], [r_bn4[h]])
            for h in range(4):
                op('vector', lambda e, h=h: e.bn_aggr(out=mv[:, h * 2:(h + 1) * 2], in_=bnst[:, h * 6:(h + 1) * 6]), [r_bn4[h]], [r_mv4[h]])
            mvv = mv.rearrange("p (h a) -> p h a", a=2)
            op('vector', lambda e: e.tensor_tensor(out=rs4, in0=mvv[:, :, 1], in1=epsT, op=ALU.add), r_mv4 + [r_tab], [r_st])
            op('vector', lambda e: e.tensor_scalar(out=negm4, in0=mvv[:, :, 0], scalar1=-1.0, scalar2=None, op0=ALU.mult), r_mv4, [r_nm])
            rsqrt_inplace(rs4, r_st)
            op('gpsimd', lambda e: e.tensor_tensor(out=nm4, in0=negm4, in1=rs4, op=ALU.mult), [r_st, r_nm], [r_nm])

        def tail2a2(t):
            b = t % 2
            ob = OOB[t % 2]
            for h in range(4):
                sl = slice(h * 128, (h + 1) * 128)
                op('scalar', lambda e, h=h, sl=sl: e.activation(out=retn[:, sl], in_=banks[ob][:, sl], func=AF.Identity,
                                                               scale=rs4[:, h:h + 1], bias=nm4[:, h:h + 1]),
                   [r_bank[ob], r_st, r_nm], [r_retn[h]])
            op('gpsimd', lambda e: e.tensor_tensor(out=prer[b], in0=retn, in1=sg[t % 4], op=ALU.mult), r_retn + [r_sg3[t % 4]], [r_prer[b]])

        def tail2b(t):
            b = t % 2
            for h in range(4):
                tr(bankb[PT][:, h * 128:(h + 1) * 128], prer[b][:, h * 128:(h + 1) * 128], identb, [r_prer[b], r_const], [r_bank[PT]], h == 3)
            op('scalar', lambda e: e.copy(out=preT[0][:, :, t * 128:(t + 1) * 128], in_=bankb[PT][:, 0:512].rearrange("p (h t) -> p h t", h=4)),
               [r_bank[PT]], [r_preT[0][t // 4]])

        def ok(i):
            return 0 <= i < 16

        def zs_mms():
            for blk, bank in ((3, ZG), (2, ZV), (1, ZK), (0, ZQ)):
                for kc in range(8):
                    mm(banks[bank][0:16, :], hT[:, kc, 2048:2064], wBv[:, kc, blk * 512:(blk + 1) * 512],
                       kc == 0, kc == 7, [r_hT[16], r_wB], [r_bank[bank]], kc == 7)

        for t in range(-1, 19):
            if t == 2:
                late_loads()
            if ok(t):
                z_blk(t, 1)
                z_blk(t, 0)
                ev_kq(t)
            if t == 16:
                zs_mms()
            if ok(t - 1):
                tail1a(t - 1)
            if ok(t + 1):
                z_blk(t + 1, 3)
                ev_g(t + 1)
            if ok(t - 3):
                tail2b(t - 3)
            if ok(t - 2):
                tail2a(t - 2)
            if ok(t - 1):
                tail1b_sc(t - 1)
            if ok(t):
                rot_k(t)
            if ok(t + 1):
                z_blk(t + 1, 2)
                ev_v(t + 1)
            if ok(t - 2):
                tail2a2(t - 2)
            if ok(t):
                rot_q(t)
            if ok(t - 1):
                tail1b_os(t - 1)
        store('sync', D['srp'].rearrange("h d e -> d h e"), s_f.rearrange("p (h e) -> p h e", h=4), [r_s])

        P.barrier()
        A.off = B_LOCAL
        GAM = [1.0 - 2.0 ** (-5.0 - h) for h in range(4)]
        cosS, sinS = cosS_p, sinS_p
        gtab16 = A.f32(512)
        delta16 = A.f32(256).rearrange("p (a b) -> p a b", a=16)
        r_t2 = Res()
        load('sync', gtab16[0:16], D['gtab16'], r_t2)
        load('sync', delta16, D['delta16'], r_t2)
        loads_done()
        q_s, k_s, v_s, sg_s, As, Bs = [A.f32(512) for _ in range(6)]
        v_sb = A.bf16(512)
        r_q, r_k, r_v, r_sgs, r_As = Res(), Res(), Res(), Res(), Res()
        qk4 = A.f32(4)
        qTs = A.f32(64).rearrange("p (h b) -> p h b", h=4)
        Qm = A.bf16(1024).rearrange("p (h b c) -> p h b c", h=4, b=16)
        r_qTs, r_Qm = Res(), Res()
        Sf = [A.f32(512).rearrange("p (h e) -> p h e", h=4) for _ in range(4)]
        r_Sf = [Res() for _ in range(4)]
        d_Sf = [P.dma_sem() for _ in range(4)]
        Sbf = [A.bf16(512) for _ in range(2)]
        r_Sbf = [Res(), Res()]
        Km = [A.bf16(512) for _ in range(2)]
        r_Km = [Res(), Res()]
        snew = [A.f32(512) for _ in range(3)]
        r_snew = [[Res() for _ in range(4)] for _ in range(3)]
        d_snew = [P.dma_sem() for _ in range(3)]
        o_s, pr_s = A.f32(512), A.f32(512)
        r_os = Res()
        bn2, mv2, rs2 = A.f32(24), A.f32(8), A.f32(4)
        def load_S(b):
            s4 = b % 4
            src = D['sret'][b].rearrange("h d e -> d h e")
            P.dma('sync', d_Sf[s4], lambda e: e.dma_start(out=Sf[s4], in_=src), [], [r_Sf[s4]])
        for b in range(3):
            load_S(b)
        op('vector', lambda e: e.tensor_scalar(out=cwTh, in0=cwT, scalar1=0.5, scalar2=None, op0=ALU.mult), [r_cv], [r_cv])
        P._deps('vector', [], [r_wB])
        P._deps('scalar', [], [r_wB])
        diag_ops = []
        idx = 0
        for k in range(31):
            for cc in range(4):
                if idx % 2 == 0:
                    diag_ops.append(('vector', lambda e, k=k, cc=cc: e.tensor_scalar(out=diag[:, k * 4 + cc, :], in0=identf, scalar1=cwTh[:, cc, k:k + 1],
                                                                                    scalar2=None, op0=ALU.mult), [r_cv, r_const], [r_diag[k * 4 + cc]]))
                else:
                    diag_ops.append(('scalar', lambda e, k=k, cc=cc: e.activation(out=diag[:, k * 4 + cc, :], in_=identf, func=AF.Copy,
                                                                                 scale=cwTh[:, cc, k:k + 1]), [r_cv, r_const], [r_diag[k * 4 + cc]]))
                idx += 1
        op('scalar', lambda e: e.activation(out=sg_s[0:16], in_=banks[ZG][0:16], func=AF.Silu), [r_bank[ZG]], [r_sgs])
        op('scalar', lambda e: e.copy(out=v_s[0:16], in_=banks[ZV][0:16]), [r_bank[ZV]], [r_v])
        op('scalar', lambda e: e.activation(out=v_sb[0:16], in_=banks[ZV][0:16], func=AF.Copy), [r_bank[ZV]], [r_v])

        def v4s(ap):
            return ap[0:16].rearrange("p (h a f) -> p h a f", h=4, a=2)

        for bank, dst, rr, scale in ((ZK, k_s, r_k, 1.0), (ZQ, q_s, r_q, 128.0 ** -0.5)):
            cs = cosS[0:16].unsqueeze(1).unsqueeze(1).to_broadcast([16, 4, 2, 64])
            zb = v4s(banks[bank])
            op('vector', lambda e, zb=zb, cs=cs: e.tensor_tensor(out=v4s(As), in0=zb, in1=cs, op=ALU.mult), [r_bank[bank], r_rotS], [r_As])
            for a in range(2):
                sn = sinS[0:16, a, :].unsqueeze(1).to_broadcast([16, 4, 64])
                op('vector', lambda e, a=a, sn=sn, zb=zb: e.tensor_tensor(out=v4s(Bs)[:, :, a, :], in0=zb[:, :, 1 - a, :], in1=sn, op=ALU.mult),
                   [r_bank[bank], r_rotS], [r_As])
            op('vector', lambda e, dst=dst: e.tensor_tensor(out=dst[0:16], in0=As[0:16], in1=Bs[0:16], op=ALU.add), [r_As], [rr])
            if scale != 1.0:
                op('vector', lambda e, dst=dst, scale=scale: e.tensor_scalar(out=dst[0:16], in0=dst[0:16], scalar1=scale, scalar2=None, op0=ALU.mult),
                   [rr], [rr])
        op('vector', lambda e: e.tensor_tensor(out=As[0:16], in0=q_s[0:16], in1=k_s[0:16], op=ALU.mult), [r_q, r_k, r_As], [r_As])
        op('vector', lambda e: e.tensor_reduce(out=qk4[0:16], in_=As[0:16].rearrange("p (h f) -> p h f", h=4), axis=AX.X, op=ALU.add),
           [r_As], [r_As])
        for h in range(4):
            tr(banks[PT][:, h * 16:(h + 1) * 16], q_s[0:16, h * 128:(h + 1) * 128], identf[0:16, 0:16], [r_q, r_const], [r_bank[PT]], h == 3)
        op('vector', lambda e: e.tensor_copy(out=qTs, in_=banks[PT][:, 0:64].rearrange("p (h b) -> p h b", h=4)), [r_bank[PT]], [r_qTs])
        op('vector', lambda e: e.tensor_tensor(out=Qm, in0=qTs.unsqueeze(3).to_broadcast([128, 4, 16, 16]),
                                               in1=delta16.unsqueeze(1).to_broadcast([128, 4, 16, 16]), op=ALU.mult),
           [r_qTs, r_t2], [r_Qm])
        OB = [SK, OO0, OO1, ZG]
        def prep(b):
            s4, s2 = b % 4, b % 2
            op('scalar', lambda e: e.activation(out=Sbf[s2], in_=Sf[s4].rearrange("p h e -> p (h e)"), func=AF.Copy),
               [r_Sf[s4]], [r_Sbf[s2]])
            op('vector', lambda e: e.tensor_scalar(out=Km[s2][0:16], in0=k_s[0:16], scalar1=identf[0:16, b:b + 1], scalar2=None,
                                                   op0=ALU.mult), [r_k, r_const], [r_Km[s2]])

        prep(0)
        for b in range(16):
            s4, s2, s3 = b % 4, b % 2, b % 3
            if b + 3 < 16:
                load_S(b + 3)
            if b + 1 < 16:
                prep(b + 1)
            for h in range(4):
                mm(banks[OB[h]][0:16, 0:128], Qm[:, h, b, :], Sbf[s2][:, h * 128:(h + 1) * 128], b == 0, b == 15, [r_Qm, r_Sbf[s2]], [r_bank[OB[h]]], True)
            bkS = ZV if b % 2 == 0 else ZK
            for h in range(4):
                sl = slice(h * 128, (h + 1) * 128)
                mm(banks[bkS][:, sl], Km[s2][0:16, sl], v_sb[0:16, sl], True, True, [r_Km[s2], r_v], [r_bank[bkS]], h == 3)
            for h in range(4):
                sl = slice(h * 128, (h + 1) * 128)
                op('vector', lambda e, h=h, sl=sl, s4=s4, s3=s3, bkS=bkS: e.scalar_tensor_tensor(
                    out=snew[s3][:, sl], in0=Sf[s4][:, h, :], scalar=GAM[h], in1=banks[bkS][:, sl], op0=ALU.mult, op1=ALU.add),
                   [r_Sf[s4], r_bank[bkS]], [r_snew[s3][h]])
            store('gpsimd', D['srs'][b].rearrange("h d e -> d h e"), snew[s3].rearrange("p (h e) -> p h e", h=4), r_snew[s3], d_snew[s3])
            for _ in range(8):
                if diag_ops:
                    op(*diag_ops.pop(0))
        while diag_ops:
            op(*diag_ops.pop(0))
        for h in range(4):
            sl = slice(h * 128, (h + 1) * 128)
            op('vector', lambda e, h=h, sl=sl: e.tensor_tensor(out=o_s[0:16, sl], in0=banks[OB[h]][0:16, 0:128], in1=gtab16[0:16, sl], op=ALU.mult),
               [r_bank[OB[h]], r_t2], [r_os])
            op('vector', lambda e, h=h, sl=sl: e.scalar_tensor_tensor(out=o_s[0:16, sl], in0=v_s[0:16, sl], scalar=qk4[0:16, h:h + 1],
                                                                      in1=o_s[0:16, sl], op0=ALU.mult, op1=ALU.add), [r_v, r_As, r_os], [r_os])
        for h in range(4):
            op('vector', lambda e, h=h: e.bn_stats(out=bn2[0:16, h * 6:(h + 1) * 6], in_=o_s[0:16, h * 128:(h + 1) * 128]), [r_os], [r_os])
        for h in range(4):
            op('vector', lambda e, h=h: e.bn_aggr(out=mv2[0:16, h * 2:(h + 1) * 2], in_=bn2[0:16, h * 6:(h + 1) * 6]), [r_os], [r_os])
        op('vector', lambda e: e.tensor_scalar(out=rs2[0:16], in0=mv2[0:16].rearrange("p (h a) -> p h a", a=2)[:, :, 1], scalar1=EPS, scalar2=None,
                                               op0=ALU.add), [r_os], [r_os])
        rsqrt_inplace(rs2[0:16], r_os)
        for h in range(4):
            sl = slice(h * 128, (h + 1) * 128)
            op('vector', lambda e, h=h, sl=sl: e.tensor_scalar(out=pr_s[0:16, sl], in0=o_s[0:16, sl], scalar1=mv2[0:16, 2 * h:2 * h + 1],
                                                             scalar2=rs2[0:16, h:h + 1], op0=ALU.subtract, op1=ALU.mult), [r_os], [r_os])
        op('vector', lambda e: e.tensor_tensor(out=pr_s[0:16], in0=pr_s[0:16], in1=sg_s[0:16], op=ALU.mult), [r_os, r_sgs], [r_os])
        for h in range(4):
            tr(banks[PT][:, 64 + h * 16:64 + (h + 1) * 16], pr_s[0:16, h * 128:(h + 1) * 128], identf[0:16, 0:16], [r_os, r_const], [r_bank[PT]], h == 3)
        for h in range(4):
            op('scalar', lambda e, h=h: e.activation(out=preT[0][:, h, 2048:2064], in_=banks[PT][:, 64 + h * 16:64 + (h + 1) * 16],
                                                     func=AF.Copy, scale=gnwT[:, h:h + 1]), [r_bank[PT], r_const], [r_preT[0][4]])

    def phase_C(wCv, r_wC, CV, wb_spare, diag_ar):
        nonlocal A
        diag, r_diag, cwT, cwTh, cbT, lnwT, lnbT, r_cv = CV
        UW = 2080
        uT = A.bf16(4 * UW).rearrange("p (c t) -> p c t", c=4)
        r_uT = [Res() for _ in range(4)]
        r_uz = Res()
        op('vector', lambda e: e.memset(uT[:, :, 0:32], 0.0), [], [r_uz])
        y = A.f32(2048).rearrange("p (c t) -> p c t", c=4)
        ybf = A.bf16(2048).rearrange("p (c t) -> p c t", c=4)
        ysq = A.bf16(2048).rearrange("p (c t) -> p c t", c=4)
        r_y = [Res() for _ in range(4)]
        r_ybf = [Res() for _ in range(4)]
        r_ysq = [Res() for _ in range(4)]
        rstd, nmr = A.f32(512), A.f32(512)
        r_stt = Res()
        sgc = [A.bf16(2048).rearrange("p (c t) -> p c t", c=4) for _ in range(2)]
        r_sgc = [[Res() for _ in range(4)] for _ in range(2)]
        tt = [wb_spare.f32(512) for _ in range(2)]
        r_tt = [Res(), Res()]
        yn = [wb_spare.f32(512), A.f32(512)]
        cs_ = [A.f32(512), A.f32(512)]
        r_yn, r_cs = [Res(), Res()], [Res(), Res()]
        nb = [0]

        def nbank():
            nb[0] = (nb[0] + 1) % 8
            return nb[0]

        def zc(j, c0, n):
            bk = nbank()
            for kc in range(8):
                mm(banks[bk][:, 0:n], wCv[:, kc, j * 128:(j + 1) * 128], hT[:, kc, c0:c0 + n], kc == 0, kc == 7,
                   [r_hT[c0 // 128 + i] for i in range(max(1, n // 128))] + [r_wC], [r_bank[bk]], kc == 7)
            return bk

        def zstage(tb):
            c0 = tb * 512
            sp = tb % 2
            for cc in range(4):
                i2 = cc % 2
                bkb = zc(4 + cc, c0, 512)
                op('scalar', lambda e, i2=i2, bkb=bkb: e.activation(out=tt[i2], in_=banks[bkb], func=AF.Tanh, scale=0.5), [r_bank[bkb]], [r_tt[i2]])
                bka = zc(cc, c0, 512)
                op('vector', lambda e, i2=i2, bka=bka, cc=cc, c0=c0: e.scalar_tensor_tensor(
                    out=uT[:, cc, 30 + c0:30 + c0 + 512], in0=tt[i2], scalar=1.0, in1=banks[bka], op0=ALU.add, op1=ALU.mult),
                   [r_tt[i2], r_bank[bka], r_uz], [r_uT[tb]])
                bkg = zc(8 + cc, c0, 512)
                op('scalar', lambda e, cc=cc, bkg=bkg, sp=sp: e.activation(out=sgc[sp][:, cc, :], in_=banks[bkg], func=AF.Silu),
                   [r_bank[bkg]], [r_sgc[sp][cc]])
            if tb == 3:
                bkb = nbank()
                for kc in range(8):
                    mm(banks[bkb], hT[:, kc, 1920:2048], wCv[:, kc, 512:1024], kc == 0, kc == 7, [r_hT[15], r_wC], [r_bank[bkb]], kc == 7)
                op('scalar', lambda e, bkb=bkb: e.activation(out=yn[0], in_=banks[bkb], func=AF.Tanh, scale=0.5), [r_bank[bkb], r_yn[0]], [r_yn[0]])
                bka = nbank()
                for kc in range(8):
                    mm(banks[bka], hT[:, kc, 1920:2048], wCv[:, kc, 0:512], kc == 0, kc == 7, [r_hT[15], r_wC], [r_bank[bka]], kc == 7)
                op('vector', lambda e, bka=bka: e.scalar_tensor_tensor(out=yn[0], in0=yn[0], scalar=1.0, in1=banks[bka], op0=ALU.add, op1=ALU.mult),
                   [r_yn[0], r_bank[bka]], [r_yn[0]])
                op('vector', lambda e: e.tensor_scalar(out=cs_[0], in0=yn[0], scalar1=0.5, scalar2=None, op0=ALU.mult), [r_yn[0], r_cs[0]], [r_cs[0]])
                store('gpsimd', D['scp'], cs_[0][98:128, :], [r_cs[0]], P.dma_sem())

        def conv(tb):
            c0 = tb * 512
            for cc in range(4):
                bk = nbank()
                for k in range(31):
                    mm(banks[bk], diag[:, k * 4 + cc, :], uT[:, cc, c0 + k:c0 + k + 512], k == 0, k == 30,
                       [r_diag[k * 4 + cc], r_uz, r_uT[tb]] + ([r_uT[tb - 1]] if tb > 0 else []), [r_bank[bk]], k == 30)
                op('scalar', lambda e, cc=cc, bk=bk: e.activation(out=y[:, cc, :], in_=banks[bk], func=AF.Identity, bias=cbT[:, cc:cc + 1], scale=1.0),
                   [r_bank[bk], r_cv], [r_y[cc]])
                op('scalar', lambda e, cc=cc, bk=bk: e.activation(out=ybf[:, cc, :], in_=banks[bk], func=AF.Identity, bias=cbT[:, cc:cc + 1], scale=1.0),
                   [r_bank[bk], r_cv], [r_ybf[cc]])
                op('scalar', lambda e, cc=cc, bk=bk: e.activation(out=ysq[:, cc, :], in_=banks[bk], func=AF.Square, bias=cbT[:, cc:cc + 1], scale=1.0),
                   [r_bank[bk], r_cv], [r_ysq[cc]])

        def stats(tb):
            bkm, bkq = nbank(), nbank()
            for cc in range(4):
                mm(banks[bkm], onesS_bf, ybf[:, cc, :], cc == 0, cc == 3, [r_const, r_ybf[cc]], [r_bank[bkm]], cc == 3)
            for cc in range(4):
                mm(banks[bkq], onesS_bf, ysq[:, cc, :], cc == 0, cc == 3, [r_const, r_ysq[cc]], [r_bank[bkq]], cc == 3)
            op('scalar', lambda e: e.activation(out=nmr, in_=banks[bkm], func=AF.Square), [r_bank[bkm], r_stt], [r_stt])
            op('vector', lambda e: e.scalar_tensor_tensor(out=rstd, in0=banks[bkq], scalar=EPS, in1=nmr, op0=ALU.add, op1=ALU.subtract),
               [r_bank[bkq], r_stt], [r_stt])
            op('scalar', lambda e: e.activation(out=rstd, in_=rstd, func=AF.Sqrt), [r_stt], [r_stt])
            op('vector', lambda e: e.reciprocal(out=rstd, in_=rstd), [r_stt], [r_stt])
            op('vector', lambda e: e.scalar_tensor_tensor(out=nmr, in0=banks[bkm], scalar=-1.0, in1=rstd, op0=ALU.mult, op1=ALU.mult),
               [r_bank[bkm], r_stt], [r_stt])

        def norm(tb):
            c0 = tb * 512
            sp = tb % 2
            for cc in range(4):
                i2 = cc % 2
                op('vector', lambda e, cc=cc, i2=i2: e.tensor_tensor(out=yn[i2], in0=y[:, cc, :], in1=rstd, op=ALU.mult), [r_y[cc], r_stt, r_yn[i2]], [r_yn[i2]])
                op('vector', lambda e, i2=i2: e.tensor_tensor(out=yn[i2], in0=yn[i2], in1=nmr, op=ALU.add), [r_yn[i2], r_stt], [r_yn[i2]])
                op('scalar', lambda e, cc=cc, i2=i2: e.activation(out=cs_[i2], in_=yn[i2], func=AF.Silu, bias=lnbT[:, cc:cc + 1], scale=lnwT[:, cc:cc + 1]),
                   [r_yn[i2], r_cv, r_cs[i2]], [r_cs[i2]])
                op('gpsimd', lambda e, cc=cc, c0=c0, i2=i2, sp=sp: e.tensor_tensor(out=preT[1][:, cc, c0:c0 + 512], in0=cs_[i2], in1=sgc[sp][:, cc, :], op=ALU.mult),
                   [r_cs[i2], r_sgc[sp][cc]], [r_preT[1][tb]])

        zstage(0)
        for tb in range(4):
            if tb + 1 < 4:
                zstage(tb + 1)
            conv(tb)
            stats(tb)
            norm(tb)

        A2 = Arena(diag_ar.ap, 8192)
        for eng_ in ('sync', 'vector', 'scalar', 'gpsimd'):
            P._deps(eng_, [], r_diag)
        A_keep = A
        A = A2
        wrep, w30b, cbb, lnwb, lnbb = [A.f32(512) for _ in range(5)]
        ind = A.f32(64).rearrange("p (i b) -> p i b", i=4)
        r_c2 = Res()
        load('sync', wrep[0:120], D['wrep'], r_c2)
        load('sync', w30b[0:16], D['w30b'], r_c2)
        load('sync', cbb[0:16], D['cbb'], r_c2)
        load('sync', lnwb[0:16], D['lnwb'], r_c2)
        load('sync', lnbb[0:16], D['lnbb'], r_c2)
        load('sync', ind[0:120], D['ind'], r_c2)
        loads_done()
        stt_ = [A.f32(512) for _ in range(4)]
        r_stt2 = [Res() for _ in range(4)]
        u_s, t1, cv, sgcs = [A.f32(512) for _ in range(4)]
        r_us, r_t1, r_cvs, r_sg2 = Res(), Res(), Res(), Res()
        bn3, mv3, rs3 = A.f32(8), A.f32(4), A.f32(2)
        sview = D['sconv'].rearrange("b k c -> (b k) c")
        for i in range(4):
            load('sync', stt_[i][0:120], sview[i * 120:(i + 1) * 120, :], r_stt2[i])
        loads_done()
        store('gpsimd', D['scs'][:, 0:29, :], D['sconv'][:, 1:30, :], [])
        bz = []
        for j in range(3):
            bk = nbank()
            for kc in range(8):
                mm(banks[bk][0:16, :], hT[:, kc, 2048:2064], wCv[:, kc, j * 512:(j + 1) * 512], kc == 0, kc == 7, [r_hT[16], r_wC], [r_bank[bk]], kc == 7)
            bz.append(bk)
        op('scalar', lambda e: e.activation(out=t1[0:16], in_=banks[bz[1]][0:16], func=AF.Tanh, scale=0.5), [r_bank[bz[1]]], [r_t1])
        op('vector', lambda e: e.scalar_tensor_tensor(out=t1[0:16], in0=t1[0:16], scalar=1.0, in1=banks[bz[0]][0:16], op0=ALU.add, op1=ALU.mult),
           [r_t1, r_bank[bz[0]]], [r_t1])
        op('vector', lambda e: e.tensor_scalar(out=u_s[0:16], in0=t1[0:16], scalar1=0.5, scalar2=None, op0=ALU.mult), [r_t1], [r_us])
        op('scalar', lambda e: e.activation(out=sgcs[0:16], in_=banks[bz[2]][0:16], func=AF.Silu), [r_bank[bz[2]]], [r_sg2])
        store('gpsimd', D['scs'][:, 29, :], u_s[0:16], [r_us])
        for i in range(4):
            op('vector', lambda e, i=i: e.tensor_tensor(out=stt_[i][0:120], in0=stt_[i][0:120], in1=wrep[0:120], op=ALU.mult), [r_stt2[i], r_c2], [r_stt2[i]])
        bk = nbank()
        for i in range(4):
            mm(banks[bk][0:16, :], ind[0:120, i, :], stt_[i][0:120], i == 0, i == 3, [r_c2, r_stt2[i]], [r_bank[bk]], i == 3)
        op('vector', lambda e: e.tensor_tensor(out=t1[0:16], in0=u_s[0:16], in1=w30b[0:16], op=ALU.mult), [r_us, r_c2, r_t1], [r_t1])
        op('vector', lambda e: e.tensor_tensor(out=t1[0:16], in0=t1[0:16], in1=cbb[0:16], op=ALU.add), [r_t1, r_c2], [r_t1])
        op('vector', lambda e, bk=bk: e.tensor_tensor(out=cv[0:16], in0=banks[bk][0:16], in1=t1[0:16], op=ALU.add), [r_bank[bk], r_t1], [r_cvs])
        op('vector', lambda e: e.bn_stats(out=bn3[0:16, 0:6], in_=cv[0:16]), [r_cvs], [r_cvs])
        op('vector', lambda e: e.bn_aggr(out=mv3[0:16, 0:2], in_=bn3[0:16, 0:6]), [r_cvs], [r_cvs])
        op('vector', lambda e: e.tensor_scalar(out=rs3[0:16, 0:1], in0=mv3[0:16, 1:2], scalar1=EPS, scalar2=None, op0=ALU.add), [r_cvs], [r_cvs])
        rsqrt_inplace(rs3[0:16, 0:1], r_cvs)
        op('vector', lambda e: e.tensor_scalar(out=cv[0:16], in0=cv[0:16], scalar1=mv3[0:16, 0:1], scalar2=rs3[0:16, 0:1], op0=ALU.subtract, op1=ALU.mult),
           [r_cvs], [r_cvs])
        op('vector', lambda e: e.tensor_tensor(out=cv[0:16], in0=cv[0:16], in1=lnwb[0:16], op=ALU.mult), [r_cvs, r_c2], [r_cvs])
        op('vector', lambda e: e.tensor_tensor(out=cv[0:16], in0=cv[0:16], in1=lnbb[0:16], op=ALU.add), [r_cvs, r_c2], [r_cvs])
        op('scalar', lambda e: e.activation(out=t1[0:16], in_=cv[0:16], func=AF.Silu), [r_cvs, r_t1], [r_t1])
        op('vector', lambda e: e.tensor_tensor(out=t1[0:16], in0=t1[0:16], in1=sgcs[0:16], op=ALU.mult), [r_t1, r_sg2], [r_t1])
        bk = nbank()
        for h in range(4):
            tr(banks[bk][:, h * 16:(h + 1) * 16], t1[0:16, h * 128:(h + 1) * 128], identf[0:16, 0:16], [r_t1, r_const], [r_bank[bk]], h == 3)
        op('scalar', lambda e, bk=bk: e.copy(out=preT[1][:, :, 2048:2064], in_=banks[bk][:, 0:64].rearrange("p (h b) -> p h b", h=4)),
           [r_bank[bk]], [r_preT[1][4]])
        A = A_keep

    def phase_D(wDj, r_wDj):
        qm_s, sgm_s = A.f32(512), A.f32(512)
        r_qms, r_sgms = Res(), Res()
        D2_LOCAL = A.off
        Kslot = [None] * 16
        r_K = [Res() for _ in range(16)]
        d_K = [P.dma_sem() for _ in range(16)]
        for i in range(6):
            Kslot[i] = tbuf[i // 3][i % 3].bitcast(BF16).rearrange("p (c f) -> p c f", c=2)
        for j in range(8):
            Kslot[6 + j] = wDj[:, j, :, :].rearrange("p k f -> p (k f)").rearrange("p (c f) -> p c f", c=2)

        def load_K(b, extra=()):
            srcK = D['ck'][b].rearrange("(c m) f -> m c f", c=2)
            P.dma('gpsimd', d_K[b], lambda e: e.dma_start(out=Kslot[b], in_=srcK), [], [r_K[b]] + list(extra))

        for b in range(6):
            load_K(b)
        qmT = A.bf16(4 * 2048).rearrange("p (h t) -> p h t", h=4)
        sgm = A.bf16(4 * 2048).rearrange("p (h t) -> p h t", h=4)
        pT_ = A.bf16(4096).rearrange("p (n t) -> p n t", n=8)
        r_qm = [[Res() for _ in range(4)] for _ in range(4)]
        r_sgm = [[Res() for _ in range(4)] for _ in range(4)]
        r_pT = [Res() for _ in range(8)]
        rs_ = [A.f32(512) for _ in range(2)]
        mT = [A.f32(512) for _ in range(2)]
        r_rs, r_mT = [Res(), Res()], [Res(), Res()]
        nb = [0]

        def nbank():
            nb[0] = (nb[0] + 1) % 8
            return nb[0]

        for j in range(8):
            for tb in range(4):
                c0 = tb * 512
                bk = nbank()
                for kc in range(8):
                    mm(banks[bk], wDj[:, j, kc, :], hT[:, kc, c0:c0 + 512], kc == 0, kc == 7,
                       [r_hT[tb * 4 + i] for i in range(4)] + [r_wDj[j]], [r_bank[bk]], kc == 7)
                if j < 4:
                    op('scalar', lambda e, j=j, bk=bk, c0=c0: e.activation(out=qmT[:, j, c0:c0 + 512], in_=banks[bk], func=AF.Copy),
                       [r_bank[bk]], [r_qm[j][tb]])
                else:
                    op('scalar', lambda e, j=j, bk=bk, c0=c0: e.activation(out=sgm[:, j - 4, c0:c0 + 512], in_=banks[bk], func=AF.Silu),
                       [r_bank[bk]], [r_sgm[j - 4][tb]])
        for j in range(2):
            bk = nbank()
            for jj in range(4):
                for kc in range(8):
                    mm(banks[bk][0:16, jj * 128:(jj + 1) * 128], hT[:, kc, 2048:2064], wDj[:, j * 4 + jj, kc, :], kc == 0, kc == 7,
                       [r_hT[16], r_wDj[j * 4 + jj]], [r_bank[bk]], kc == 7)
            if j == 0:
                op('scalar', lambda e, bk=bk: e.copy(out=qm_s[0:16], in_=banks[bk][0:16]), [r_bank[bk]], [r_qms])
            else:
                op('scalar', lambda e, bk=bk: e.activation(out=sgm_s[0:16], in_=banks[bk][0:16], func=AF.Silu), [r_bank[bk]], [r_sgms])
        for j in range(8):
            load_K(6 + j, extra=[r_wDj[j]])
        for tb in range(4):
            c0 = tb * 512
            for h in range(4):
                for mc in range(2):
                    bk = nbank()
                    mm(banks[bk], kT[:, h, mc * 128:(mc + 1) * 128], qmT[:, h, c0:c0 + 512], True, True, [r_kv, r_qm[h][tb]], [r_bank[bk]], True)
                    op('scalar', lambda e, h=h, mc=mc, bk=bk: e.activation(out=pT_[:, mc * 4 + h, :], in_=banks[bk], func=AF.Exp),
                       [r_bank[bk]], [r_pT[mc * 4 + h]])
            for h in range(4):
                i2 = h % 2
                bko, bkd = nbank(), nbank()
                for mc in range(2):
                    mm(banks[bko], vmem[:, mc, h * 128:(h + 1) * 128], pT_[:, mc * 4 + h, :], mc == 0, mc == 1, [r_kv, r_pT[mc * 4 + h]], [r_bank[bko]], mc == 1)
                for mc in range(2):
                    mm(banks[bkd], ones_bf, pT_[:, mc * 4 + h, :], mc == 0, mc == 1, [r_const, r_pT[mc * 4 + h]], [r_bank[bkd]], mc == 1)
                op('scalar', lambda e, i2=i2, bkd=bkd: e.activation(out=rs_[i2], in_=banks[bkd], func=AF.Ln), [r_bank[bkd]], [r_rs[i2]])
                op('scalar', lambda e, i2=i2: e.activation(out=rs_[i2], in_=rs_[i2], func=AF.Exp, scale=-1.0), [r_rs[i2]], [r_rs[i2]])
                op('vector', lambda e, i2=i2, bko=bko: e.tensor_tensor(out=mT[i2], in0=banks[bko], in1=rs_[i2], op=ALU.mult),
                   [r_bank[bko], r_rs[i2]], [r_mT[i2]])
                op('vector', lambda e, i2=i2, h=h, c0=c0: e.tensor_tensor(out=preT[2][:, h, c0:c0 + 512], in0=mT[i2], in1=sgm[:, h, c0:c0 + 512], op=ALU.mult),
                   [r_mT[i2], r_sgm[h][tb]], [r_preT[2][tb]])

        P.barrier()
        A.off = D2_LOCAL
        delta16 = A.f32(256).rearrange("p (a b) -> p a b", a=16)
        r_d2 = Res()
        load('sync', delta16, D['delta16'], r_d2)
        loads_done()
        pm_s = A.f32(512)
        r_pms = Res()
        qmTs = A.f32(64).rearrange("p (h b) -> p h b", h=4)
        QmM = A.bf16(1024).rearrange("p (h b c) -> p h b c", h=4, b=16)
        r_qmTs, r_QmM = Res(), Res()
        NV = 6
        for i in (14, 15):
            Kslot[i] = A.bf16(1024).rearrange("p (c f) -> p c f", c=2)
        Vb = [A.bf16(1024).rearrange("p (c f) -> p c f", c=2) for _ in range(NV)]
        r_Vb = [Res() for _ in range(NV)]
        d_Vb = [P.dma_sem() for _ in range(NV)]
        KbT = [A.bf16(1024).rearrange("p (h m) -> p h m", h=4) for _ in range(2)]
        r_KbT = [Res(), Res()]
        p_s = A.f32(1024).rearrange("p (h m) -> p h m", h=4)
        r_ps = Res()
        den, rden = A.f32(4), A.f32(4)
        r_den = Res()
        pTs = A.f32(128).rearrange("p (n b) -> p n b", n=8)
        PTm = A.bf16(2048).rearrange("p (n b c) -> p n b c", n=8, b=16)
        r_pTs, r_PTm = Res(), Res()

        def vslot(b):
            if b < NV:
                return Vb[b], r_Vb[b], d_Vb[b]
            return Kslot[b - NV], r_K[b - NV], d_K[b - NV]

        def load_V(b):
            dst, res, dsm = vslot(b)
            srcV = D['cv'][b].rearrange("(c m) f -> m c f", c=2)
            P.dma('gpsimd', dsm, lambda e: e.dma_start(out=dst, in_=srcV), [], [res])

        load_K(14)
        load_K(15)
        for b in range(NV):
            load_V(b)
        TB = [0, 1]
        SB = [2, 3, 4, 5]
        XB = 6
        for h in range(4):
            tr(banks[XB][:, h * 16:(h + 1) * 16], qm_s[0:16, h * 128:(h + 1) * 128], identf[0:16, 0:16], [r_qms, r_const], [r_bank[XB]], h == 3)
        op('vector', lambda e: e.tensor_copy(out=qmTs, in_=banks[XB][:, 0:64].rearrange("p (h b) -> p h b", h=4)), [r_bank[XB]], [r_qmTs])
        op('vector', lambda e: e.tensor_tensor(out=QmM, in0=qmTs.unsqueeze(3).to_broadcast([128, 4, 16, 16]),
                                               in1=delta16.unsqueeze(1).to_broadcast([128, 4, 16, 16]), op=ALU.mult),
           [r_qmTs, r_d2], [r_QmM])

        def trK(b):
            s2 = b % 2
            tb_ = TB[s2]
            for h in range(4):
                for mc in range(2):
                    tr(bankb[tb_][:, h * 256 + mc * 128:h * 256 + (mc + 1) * 128], Kslot[b][:, mc, h * 128:(h + 1) * 128], identb,
                       [r_K[b], r_const], [r_bank[tb_]], h == 3 and mc == 1)
            op('scalar', lambda e: e.copy(out=KbT[s2].rearrange("p h m -> p (h m)"), in_=bankb[tb_]), [r_bank[tb_]], [r_KbT[s2]])
            if b + NV < 16:
                load_V(b + NV)

        trK(0)
        for b in range(16):
            s2 = b % 2
            if b + 1 < 16:
                trK(b + 1)
            for h in range(4):
                mm(banks[SB[h]][0:16, 0:256], QmM[:, h, b, :], KbT[s2][:, h, :], b == 0, b == 15, [r_QmM, r_KbT[s2]], [r_bank[SB[h]]], True)
        for h in range(4):
            op('scalar', lambda e, h=h: e.activation(out=p_s[0:16, h, :], in_=banks[SB[h]][0:16, 0:256], func=AF.Exp, scale=128.0 ** -0.5,
                                                     accum_out=den[0:16, h:h + 1]), [r_bank[SB[h]]], [r_ps])
        op('vector', lambda e: e.reciprocal(out=rden[0:16], in_=den[0:16]), [r_ps], [r_den])
        for h in range(4):
            for mc in range(2):
                tr(banks[XB][:, 64 + (mc * 4 + h) * 16:64 + (mc * 4 + h + 1) * 16], p_s[0:16, h, mc * 128:(mc + 1) * 128], identf[0:16, 0:16],
                   [r_ps, r_const], [r_bank[XB]], h == 3 and mc == 1)
        op('vector', lambda e: e.tensor_copy(out=pTs, in_=banks[XB][:, 64:192].rearrange("p (n b) -> p n b", n=8)), [r_bank[XB]], [r_pTs])
        op('vector', lambda e: e.tensor_tensor(out=PTm, in0=pTs.unsqueeze(3).to_broadcast([128, 8, 16, 16]),
                                               in1=delta16.unsqueeze(1).to_broadcast([128, 8, 16, 16]), op=ALU.mult), [r_pTs, r_d2], [r_PTm])
        for b in range(16):
            vs, vres, _ = vslot(b)
            for h in range(4):
                for mc in range(2):
                    mm(banks[SB[h]][0:16, 256:384], PTm[:, mc * 4 + h, b, :], vs[:, mc, h * 128:(h + 1) * 128],
                       b == 0 and mc == 0, b == 15 and mc == 1, [r_PTm, vres], [r_bank[SB[h]]], True)
        for h in range(4):
            sl = slice(h * 128, (h + 1) * 128)
            op('vector', lambda e, h=h, sl=sl: e.tensor_scalar(out=pm_s[0:16, sl], in0=banks[SB[h]][0:16, 256:384], scalar1=rden[0:16, h:h + 1], scalar2=None,
                                                             op0=ALU.mult), [r_bank[SB[h]], r_den], [r_pms])
        op('vector', lambda e: e.tensor_tensor(out=pm_s[0:16], in0=pm_s[0:16], in1=sgm_s[0:16], op=ALU.mult), [r_pms, r_sgms], [r_pms])
        bk = 7
        for h in range(4):
            tr(banks[bk][:, h * 16:(h + 1) * 16], pm_s[0:16, h * 128:(h + 1) * 128], identf[0:16, 0:16], [r_pms, r_const], [r_bank[bk]], h == 3)
        op('scalar', lambda e, bk=bk: e.copy(out=preT[2][:, :, 2048:2064], in_=banks[bk][:, 0:64].rearrange("p (h b) -> p h b", h=4)),
           [r_bank[bk]], [r_preT[2][4]])

    def phase_E(wOv, r_wO, EB):
        mergedT = A.bf16(8 * NTOK).rearrange("p (k t) -> p k t", k=8)
        r_mg = [Res() for _ in range(5)]
        xst = [A.f32(1024) for _ in range(4)]
        r_xst = [Res() for _ in range(4)]
        d_xst = [P.dma_sem() for _ in range(4)]
        fnwb = A.f32(1024)
        ss, rstd = A.f32(20), A.f32(20)
        r_ss = [Res() for _ in range(20)]
        r_fn = Res()
        load('sync', fnwb, D['final_norm_w'].partition_broadcast(128), r_fn)
        loads_done()
        wg, wbr, r_wg, r_wbr, tbuf, junk, load_fc = EB
        r_junkE = Res()
        r_t = [[Res() for _ in range(3)] for _ in range(2)]
        nb = [0]

        def nbank():
            nb[0] = (nb[0] + 1) % 8
            return nb[0]

        tiles = [(D['xp'][t * 128:(t + 1) * 128, :], D['yp'][t * 128:(t + 1) * 128, :], 128, t * 128, t // 4) for t in range(16)]
        tiles.append((D['xs'], D['ys'], 16, 2048, 4))

        def ldx(i):
            src, dst, nr, c0, bi = tiles[i]
            s = i % 4
            xt = xst[s][0:nr]
            P.dma('sync', d_xst[s], lambda e: e.dma_start(out=xt, in_=src), [], [r_xst[s]])

        for i in range(4):
            ldx(i)
        for kc in range(4):
            op('vector', lambda e, kc=kc: e.tensor_scalar(out=preT[0][:, kc, 0:2048], in0=preT[0][:, kc, 0:2048], scalar1=gnwT[:, kc:kc + 1],
                                                         scalar2=None, op0=ALU.mult), [r_const] + r_preT[0][0:4], r_preT[0][0:4])
        blocks = [(tb * 512, 512, tb) for tb in range(4)] + [(2048, 16, 4)]
        it = 0
        for fc in range(8):
            s = fc % 2
            if fc + 1 < 8:
                load_fc(fc + 1)
            for (c0, n, bi) in blocks:
                hres = [r_hT[16]] if bi == 4 else [r_hT[bi * 4 + i] for i in range(4)]
                ts_ = it % 2
                it += 1
                for g in range(3):
                    bkg = nbank()
                    for kc in range(8):
                        mm(banks[bkg][:, 0:n], wg[s][:, g, kc, :], hT[:, kc, c0:c0 + n], kc == 0, kc == 7, hres + [r_wg[s]], [r_bank[bkg]], kc == 7)
                    tg = tbuf[ts_][g][:, 0:n]
                    op('scalar', lambda e, tg=tg, bkg=bkg, n=n: e.activation(out=tg, in_=banks[bkg][:, 0:n], func=AF.Tanh, scale=0.5),
                       [r_bank[bkg]], [r_t[ts_][g]])
                    bkb = nbank()
                    for kc in range(4):
                        mm(banks[bkb][:, 0:n], wbr[s][:, g, kc, :], preT[g][:, kc, c0:c0 + n], kc == 0, kc == 3, [r_preT[g][bi], r_wbr[s]], [r_bank[bkb]], kc == 3)
                    op('vector', lambda e, tg=tg, bkb=bkb, n=n: e.scalar_tensor_tensor(out=tg, in0=tg, scalar=1.0, in1=banks[bkb][:, 0:n],
                                                                                        op0=ALU.add, op1=ALU.mult), [r_t[ts_][g], r_bank[bkb]], [r_t[ts_][g]])
                t0, t1, t2 = [tbuf[ts_][g][:, 0:n] for g in range(3)]
                op('gpsimd', lambda e, t0=t0, t1=t1: e.tensor_tensor(out=t0, in0=t0, in1=t1, op=ALU.add), [r_t[ts_][0], r_t[ts_][1]], [r_t[ts_][0]])
                op('gpsimd', lambda e, t0=t0, t2=t2, fc=fc, c0=c0, n=n: e.tensor_tensor(out=mergedT[:, fc, c0:c0 + n], in0=t0, in1=t2, op=ALU.add),
                   [r_t[ts_][0], r_t[ts_][2]], [r_mg[bi]])
        def e1(i):
            src, dst, nr, c0, bi = tiles[i]
            s = i % 4
            xt = xst[s][0:nr]
            for half in range(2):
                bk = nbank()
                for kc in range(8):
                    mm(banks[bk][0:nr, :], mergedT[:, kc, c0:c0 + nr], wOv[:, kc, half * 512:(half + 1) * 512], kc == 0, kc == 7,
                       [r_mg[bi], r_wO], [r_bank[bk]], kc == 7)
                xh = xt[:, half * 512:(half + 1) * 512]
                op('vector', lambda e, xh=xh, bk=bk: e.scalar_tensor_tensor(out=xh, in0=banks[bk][0:nr, :], scalar=0.5, in1=xh,
                                                                          op0=ALU.mult, op1=ALU.add), [r_bank[bk], r_xst[s]], [r_xst[s]])
            ssi, rsi = ss[0:nr, i:i + 1], rstd[0:nr, i:i + 1]
            op('scalar', lambda e: e.activation(out=junk[0:nr], in_=xt, func=AF.Square, accum_out=ssi), [r_xst[s]], [r_ss[i], r_junkE])
            op('gpsimd', lambda e: e.tensor_scalar(out=rsi, in0=ssi, scalar1=1.0 / 1024, scalar2=EPS, op0=ALU.mult, op1=ALU.add),
               [r_ss[i]], [r_ss[i]])
            rsqrt_inplace(rsi, r_ss[i])

        def e2(i):
            src, dst, nr, c0, bi = tiles[i]
            s = i % 4
            xt = xst[s][0:nr]
            rsi = rstd[0:nr, i:i + 1]
            op('vector', lambda e: e.scalar_tensor_tensor(out=xt, in0=xt, scalar=rsi, in1=fnwb[0:nr], op0=ALU.mult, op1=ALU.mult),
               [r_xst[s], r_ss[i], r_fn], [r_xst[s]])
            store('gpsimd', dst, xt, [r_xst[s]], d_xst[s])

        n = len(tiles)
        e1(0)
        for i in range(n):
            if i + 1 < n:
                e1(i + 1)
            e2(i)
            if i + 4 < n:
                ldx(i + 4)

    B_LOCAL = C_LOCAL = D_LOCAL = PERS
    WA.off = 0
    wBv = WA.bf16(8 * 2048).rearrange("p (k n) -> p k n", k=8)
    r_wB = Res()
    WB.off = 0
    wkv = WB.f32(4096).bitcast(BF16).rearrange("p (k n) -> p k n", k=8)
    r_wkv = Res()
    load_wkv = lambda: load_w(wkv, D['w_mem_kv'].rearrange("(k p) n -> p k n", p=128), r_wkv, nsplit=2)
    def load_wB():
        g = LoadGroup()
        for blk in (3, 2, 1, 0):
            sl = slice(blk * 512, (blk + 1) * 512)
            g.add('gpsimd', wBv[:, :, sl], w_in_v[:, :, sl], r_wB)
        g.done()

    phase_A(wkv, r_wkv, load_wB, load_wkv)
    P.barrier()
    A.off = PERS
    WB.off = 0
    wCv = WB.bf16(8 * 1536).rearrange("p (k n) -> p k n", k=8)
    r_wC = Res()
    cwT = WB.f32(124).rearrange("p (c k) -> p c k", c=4)
    cwTh = WB.f32(124).rearrange("p (c k) -> p c k", c=4)
    cbT, lnwT, lnbT = WB.f32(4), WB.f32(4), WB.f32(4)
    cosS_p = WB.f32(64)
    sinS_p = WB.f32(128).rearrange("p (a f) -> p a f", a=2)
    r_rotS = Res()
    diag = Arena(WA.ap, 8192).bf16(124 * 128).rearrange("p (n c) -> p n c", n=124)
    CV = (diag, [Res() for _ in range(124)], cwT, cwTh, cbT, lnwT, lnbT, Res())
    phase_B(wBv, r_wB, CV, lambda: load_w(wCv, w_in_v[:, :, 2048:3584], r_wC, nsplit=4))
    P.barrier()
    A.off = PERS
    WA.off = 0
    phase_C(wCv, r_wC, CV, WB, WA)
    P.barrier()
    A.off = PERS
    WA.off = 0
    WB.off = 0
    wDj = WA.bf16(8 * 8 * 128).rearrange("p (j k f) -> p j k f", j=8, k=8)
    wOv = WA.bf16(8 * 1024).rearrange("p (k n) -> p k n", k=8)
    r_wDj = [Res() for _ in range(8)]
    r_wO = Res()
    gD = LoadGroup()
    for j in range(8):
        c0 = 3584 + j * 128
        dsm = P.dma_sem()
        P.dma('gpsimd', dsm, lambda e, j=j, c0=c0: e.dma_start(out=wDj[:, j, :, :], in_=w_in_v[:, :, c0:c0 + 128]), [], [r_wDj[j]])
    wg = [WB.bf16(3 * 8 * 128).rearrange("p (g k f) -> p g k f", g=3, k=8) for _ in range(2)]
    wbr = [WB.bf16(3 * 4 * 128).rearrange("p (g k f) -> p g k f", g=3, k=4) for _ in range(2)]
    r_wg, r_wbr = [Res(), Res()], [Res(), Res()]
    tbuf = [[WB.f32(512) for _ in range(3)] for _ in range(2)]
    junkE = WB.bf16(1024)
    brw = [D['w_br_ret'], D['w_br_conv'], D['w_br_mem']]
    d_wg, d_wbr = [P.dma_sem(), P.dma_sem()], [P.dma_sem(), P.dma_sem()]

    def load_fc(fc):
        s_ = fc % 2
        for g in range(3):
            c0 = 4608 + g * 1024 + fc * 128
            P.dma('gpsimd', d_wg[s_], lambda e, s_=s_, g=g, c0=c0: e.dma_start(out=wg[s_][:, g, :, :], in_=w_in_v[:, :, c0:c0 + 128]), [], [r_wg[s_]])
        r_wg[s_].w = (id(d_wg[s_].sem), d_wg[s_].count)
        for g in range(3):
            src = brw[g].rearrange("(k p) n -> p k n", p=128)[:, :, fc * 128:(fc + 1) * 128]
            P.dma('gpsimd', d_wbr[s_], lambda e, s_=s_, g=g, src=src: e.dma_start(out=wbr[s_][:, g, :, :], in_=src), [], [r_wbr[s_]])
        r_wbr[s_].w = (id(d_wbr[s_].sem), d_wbr[s_].count)

    load_w(wOv, D['w_out'].rearrange("(k p) n -> p k n", p=128), r_wO, nsplit=2)
    load_fc(0)
    phase_D(wDj, r_wDj)
    P.barrier()
    A.off = PERS
    phase_E(wOv, r_wO, (wg, wbr, r_wg, r_wbr, tbuf, junkE, load_fc))
    P.barrier(engines=['sync'])
    P.replay()
    return nc


_CACHE = {}


def kernel(x_prompt, x_sample, mem_prompt, state_ret, state_conv, cache_mem_k, cache_mem_v,
           norm_w, w_in, ret_gn_w, conv_w, conv_b, conv_ln_w, conv_ln_b, mem_norm_w,
           w_mem_kv, w_br_ret, w_br_conv, w_br_mem, w_out, final_norm_w):
    f = lambda a: np.ascontiguousarray(np.asarray(a, dtype=np.float32))
    x_prompt, x_sample, mem_prompt = f(x_prompt), f(x_sample), f(mem_prompt)
    state_ret, state_conv, cache_mem_k, cache_mem_v = f(state_ret), f(state_conv), f(cache_mem_k), f(cache_mem_v)
    if 'nc' not in _CACHE:
        _CACHE['nc'] = build_program()
        _CACHE['consts'] = make_consts()
    nc, consts = _CACHE['nc'], _CACHE['consts']

    def colT(v):
        return np.ascontiguousarray(f(v).reshape(4, 128).T)

    cw = f(conv_w)[0]
    shared = {
        'norm_w': f(norm_w)[0], 'w_in': f(w_in)[0], 'gnwT': colT(f(ret_gn_w)[0]),
        'cwT': np.ascontiguousarray(cw.T.reshape(4, 128, 31).transpose(1, 0, 2)),
        'wrep': np.ascontiguousarray(np.tile(cw[0:30], (4, 1))),
        'w30b': np.ascontiguousarray(np.broadcast_to(cw[30][None], (16, 512))),
        'cbb': np.ascontiguousarray(np.broadcast_to(f(conv_b)[0][None], (16, 512))),
        'cbT': colT(f(conv_b)[0]), 'lnwT': colT(f(conv_ln_w)[0]), 'lnbT': colT(f(conv_ln_b)[0]),
        'lnwb': np.ascontiguousarray(np.broadcast_to(f(conv_ln_w)[0][None], (16, 512))),
        'lnbb': np.ascontiguousarray(np.broadcast_to(f(conv_ln_b)[0][None], (16, 512))),
        'mem_norm_w': f(mem_norm_w)[0], 'w_mem_kv': f(w_mem_kv)[0],
        'w_br_ret': f(w_br_ret)[0], 'w_br_conv': f(w_br_conv)[0], 'w_br_mem': f(w_br_mem)[0],
        'w_out': f(w_out)[0], 'final_norm_w': f(final_norm_w),
    }
    for k, v in consts.items():
        shared['c_' + k] = v
    in_maps = []
    for i in range(8):
        m = dict(shared)
        sl = slice(16 * i, 16 * i + 16)
        m['xp'] = x_prompt[i]
        m['xs'] = np.ascontiguousarray(x_sample[sl, 0, :])
        m['memp'] = mem_prompt[i]
        m['sret'] = state_ret[0, sl]
        m['sconv'] = state_conv[0, sl]
        m['ck'] = np.ascontiguousarray(cache_mem_k[0, sl].reshape(16, 256, 512))
        m['cv'] = np.ascontiguousarray(cache_mem_v[0, sl].reshape(16, 256, 512))
        in_maps.append(m)
    res = run_bass_kernel_spmd(nc, in_maps, core_ids=list(range(8)))
    R = res.results
    y_prompt = np.stack([R[i]['yp'] for i in range(8)], axis=0)
    y_sample = np.concatenate([R[i]['ys'] for i in range(8)], axis=0)[:, None, :]
    srp = np.stack([R[i]['srp'] for i in range(8)], axis=0)[None]
    srs = np.concatenate([R[i]['srs'] for i in range(8)], axis=0)[None]
    scp = np.stack([R[i]['scp'] for i in range(8)], axis=0)[None]
    scs = np.concatenate([R[i]['scs'] for i in range(8)], axis=0)[None]
    mkp = np.stack([R[i]['mkp'].reshape(256, 4, 128) for i in range(8)], axis=0)[None]
    mvp = np.stack([R[i]['mvp'].reshape(256, 4, 128) for i in range(8)], axis=0)[None]
    return (y_prompt.astype(np.float32), y_sample.astype(np.float32), srp.astype(np.float32), srs.astype(np.float32),
            scp.astype(np.float32), scs.astype(np.float32), mkp.astype(np.float32), mvp.astype(np.float32))
```

```python
import numpy as np
import concourse.bass as bass
import concourse.mybir as mybir
from concourse.bass_utils import run_bass_kernel_spmd

F32 = mybir.dt.float32
BF16 = mybir.dt.bfloat16
ALU = mybir.AluOpType
AF = mybir.ActivationFunctionType
AX = mybir.AxisListType

ENGS = ['tensor', 'vector', 'scalar', 'gpsimd', 'sync']
EPS = 1e-6
NTOK = 2064


class Res:
    __slots__ = ('w', 'r')

    def __init__(self):
        self.w = None
        self.r = []


class DmaSem:
    def __init__(self, sem):
        self.sem = sem
        self.count = 0


class Prog:
    def __init__(self, nc):
        self.nc = nc
        self.sem, self.n, self.ops, self.waited = {}, {}, {}, {}
        self.semobjs = {}
        for e in ENGS:
            self.sem[e] = nc.alloc_semaphore(name=f"prog_{e}")
            self.n[e] = 0
            self.ops[e] = []
            self.waited[e] = {}
            self.semobjs[id(self.sem[e])] = self.sem[e]
        self.dsems = []

    def dma_sem(self):
        s = self.nc.alloc_semaphore(name=f"dsem{len(self.dsems)}")
        self.semobjs[id(s)] = s
        d = DmaSem(s)
        self.dsems.append(d)
        return d

    def _wait(self, eng, ev):
        semid, val = ev
        if self.waited[eng].get(semid, 0) >= val:
            return
        self.waited[eng][semid] = val
        self.ops[eng].append(('wait', self.semobjs[semid], val))

    def _deps(self, eng, reads, writes, own2=None):
        own = id(self.sem[eng])
        for r in reads:
            if r.w is not None:
                self._wait(eng, r.w)
        skip_own = (eng == 'tensor')
        for w in writes:
            if w.w is not None and w.w[0] != own2 and not (skip_own and w.w[0] == own):
                self._wait(eng, w.w)
            for ev in w.r:
                if not (skip_own and ev[0] == own):
                    self._wait(eng, ev)

    def op(self, eng, fn, reads=(), writes=(), inc=True):
        self._deps(eng, reads, writes)
        ev = (id(self.sem[eng]), self.n[eng] + 1)
        if inc:
            self.n[eng] += 1
            self.ops[eng].append(('op', fn, self.sem[eng], 1))
        else:
            self.ops[eng].append(('op', fn, None, 0))
        for r in reads:
            r.r.append(ev)
        for w in writes:
            w.w = ev
            w.r = []
        return ev

    def dma(self, eng, dsem, fn, reads=(), writes=()):
        self._deps(eng, reads, writes, own2=id(dsem.sem))
        dsem.count += 16
        ev = (id(dsem.sem), dsem.count)
        self.ops[eng].append(('op', fn, dsem.sem, 16))
        for r in reads:
            r.r.append(ev)
        for w in writes:
            w.w = ev
            w.r = []
        return ev

    def barrier(self, engines=('vector', 'scalar', 'gpsimd', 'sync')):
        for e in engines:
            for o in ENGS:
                if o != e and self.n[o] > 0:
                    self._wait(e, (id(self.sem[o]), self.n[o]))
            for d in self.dsems:
                if d.count > 0:
                    self._wait(e, (id(d.sem), d.count))

    def replay(self):
        with self.nc.Block() as block:
            for e in ENGS:
                ops = self.ops[e]

                def body(engobj, ops=ops):
                    for item in ops:
                        if item[0] == 'wait':
                            engobj.wait_ge(item[1], item[2])
                        else:
                            ins = item[1](engobj)
                            if item[2] is not None:
                                ins.then_inc(item[2], item[3])
                getattr(block, e)(body)


class Arena:
    def __init__(self, ap, nwords):
        self.ap = ap
        self.n = nwords
        self.off = 0

    def f32(self, n):
        assert self.off + n <= self.n, ("arena overflow", self.off, n, self.n)
        a = self.ap[:, self.off:self.off + n]
        self.off += n
        return a

    def bf16(self, n):
        assert n % 2 == 0
        return self.f32(n // 2).bitcast(BF16)


def make_consts():
    f32 = np.float32
    c = {}
    c['identf'] = np.eye(128, dtype=f32)
    j = np.arange(128)
    c['cmask'] = (j[None, :] >= j[:, None]).astype(f32)
    inv = (10000.0 ** (-np.arange(64, dtype=f32) / f32(64))).astype(f32)

    def rot(pos):
        ang = (pos.astype(f32)[:, None] * inv[None, :]).astype(f32).astype(np.float64)
        cs = np.cos(ang).astype(f32)
        sn = np.sin(ang).astype(f32)
        return cs, np.stack([-sn, sn], axis=1)

    cs, sn2 = rot(np.arange(2048))
    c['cosP'] = np.ascontiguousarray(cs.reshape(16, 128, 64).transpose(1, 0, 2))
    c['sinP'] = np.ascontiguousarray(sn2.reshape(16, 128, 2, 64).transpose(1, 0, 2, 3))
    cs, sn2 = rot(np.full((16,), 16384.0))
    c['cosS'] = cs
    c['sinS'] = sn2
    g = (1.0 - 2.0 ** (-5.0 - np.arange(4, dtype=np.float64)))
    jj = np.arange(128, dtype=np.float64)
    c['gk'] = (g[None, :] ** (-(jj[:, None] + 1.0))).astype(f32)
    c['epsT'] = (EPS * 128.0 / (g[None, :] ** (2.0 * (jj[:, None] + 1.0)))).astype(f32)
    c['ctab'] = np.repeat((g ** 128.0)[None, :], 128, axis=0).repeat(128, axis=1).astype(f32)
    c['gtab16'] = np.repeat(g[None, :], 16, axis=0).repeat(128, axis=1).astype(f32)
    gI = np.zeros((128, 4, 128), f32)
    for h in range(4):
        gI[:, h, :] = np.eye(128) * g[h]
    c['gI'] = gI
    c['delta16'] = np.ascontiguousarray(np.broadcast_to(np.eye(16, dtype=f32)[None], (128, 16, 16)))
    ind = np.zeros((120, 4, 16), f32)
    for i in range(4):
        for bb in range(4):
            ind[bb * 30:(bb + 1) * 30, i, 4 * i + bb] = 1.0
    c['ind'] = ind
    sel = np.zeros((16, 16, 128), f32)
    for b in range(16):
        sel[b, b, :] = 1.0
    c['sel'] = sel
    return c


CONST_SHAPES = {
    'identf': [128, 128], 'cmask': [128, 128], 'cosP': [128, 16, 64], 'sinP': [128, 16, 2, 64],
    'cosS': [16, 64], 'sinS': [16, 2, 64], 'gk': [128, 4], 'epsT': [128, 4], 'ctab': [128, 512],
    'gtab16': [16, 512], 'gI': [128, 4, 128], 'delta16': [128, 16, 16], 'ind': [120, 4, 16],
    'sel': [16, 16, 128],
}

IN_SHAPES = {
    'xp': [2048, 1024], 'xs': [16, 1024], 'memp': [256, 1024], 'sret': [16, 4, 128, 128],
    'sconv': [16, 30, 512], 'ck': [16, 256, 512], 'cv': [16, 256, 512],
    'norm_w': [1024], 'w_in': [1024, 7680], 'gnwT': [128, 4], 'cwT': [128, 4, 31], 'wrep': [120, 512],
    'w30b': [16, 512], 'cbb': [16, 512], 'cbT': [128, 4], 'lnwT': [128, 4], 'lnbT': [128, 4],
    'lnwb': [16, 512], 'lnbb': [16, 512], 'mem_norm_w': [1024], 'w_mem_kv': [1024, 1024],
    'w_br_ret': [512, 1024], 'w_br_conv': [512, 1024], 'w_br_mem': [512, 1024], 'w_out': [1024, 1024],
    'final_norm_w': [1024],
}
OUT_SHAPES = {
    'yp': [2048, 1024], 'ys': [16, 1024], 'srp': [4, 128, 128], 'srs': [16, 4, 128, 128],
    'scp': [30, 512], 'scs': [16, 30, 512], 'mkp': [256, 512], 'mvp': [256, 512],
}


def build_program():
    nc = bass.Bass("TRN2", target_bir_lowering=False)
    D = {}
    for k, s in IN_SHAPES.items():
        D[k] = nc.dram_tensor(k, s, F32, kind="ExternalInput").ap()
    for k, s in CONST_SHAPES.items():
        D[k] = nc.dram_tensor("c_" + k, s, F32, kind="ExternalInput").ap()
    for k, s in OUT_SHAPES.items():
        D[k] = nc.dram_tensor(k, s, F32, kind="ExternalOutput").ap()

    P = Prog(nc)
    NW = 52992
    arena_ap = nc.alloc_sbuf_tensor("arena", [128, NW], F32).ap()
    A = Arena(arena_ap, NW)
    banks = [nc.alloc_psum_tensor(f"bank{i}", [128, 512], F32).ap() for i in range(8)]
    bankb = [b.bitcast(BF16) for b in banks]
    r_bank = [Res() for _ in range(8)]

    def op(eng, fn, reads=(), writes=(), inc=True):
        return P.op(eng, fn, reads, writes, inc)

    def mm(out, lhsT, rhs, start, stop, reads, writes, inc):
        P.op('tensor', lambda e: e.matmul(out, lhsT=lhsT, rhs=rhs, start=start, stop=stop), reads, writes, inc)

    def tr(out, in_, ident, reads, writes, inc):
        P.op('tensor', lambda e: e.transpose(out=out, in_=in_, identity=ident), reads, writes, inc)

    out_sem = P.dma_sem()
    ld_sem_n = [0]

    class LoadGroup:
        def __init__(self):
            self.d = P.dma_sem()
            self.res = []

        def add(self, eng, out, in_, res):
            P.dma(eng, self.d, lambda e: e.dma_start(out=out, in_=in_), reads=[], writes=[res])
            self.res.append(res)

        def done(self):
            ev = (id(self.d.sem), self.d.count)
            for r in self.res:
                r.w = ev

    cur_grp = [None]

    def load(eng, out, in_, res):
        if cur_grp[0] is None:
            cur_grp[0] = LoadGroup()
        cur_grp[0].add(eng, out, in_, res)

    def loads_done():
        if cur_grp[0] is not None:
            cur_grp[0].done()
            cur_grp[0] = None

    def store(eng, out, in_, reads, dsem=None):
        P.dma(eng, out_sem if dsem is None else dsem, lambda e: e.dma_start(out=out, in_=in_), reads=reads)

    hT = A.bf16(8 * NTOK).rearrange("p (k t) -> p k t", k=8)
    r_hT = [Res() for _ in range(17)]
    preT = [A.bf16(4 * NTOK).rearrange("p (k t) -> p k t", k=4) for _ in range(3)]
    r_preT = [[Res() for _ in range(5)] for _ in range(3)]
    WA = Arena(A.f32(8192), 8192)
    WB = Arena(A.f32(8192), 8192)
    identf = A.f32(128)
    identb = A.bf16(128)
    ones_bf = A.bf16(128)
    onesS_bf = A.bf16(128)
    ones_f = A.f32(1)
    mhalf = A.f32(512)
    kT = A.bf16(4 * 256).rearrange("p (h m) -> p h m", h=4)
    vmem = A.bf16(2 * 512).rearrange("p (c f) -> p c f", c=2)
    gnwT = A.f32(4)
    r_const = Res()
    r_kv = Res()
    PERS = A.off

    load('sync', identf, D['identf'], r_const)
    load('sync', gnwT, D['gnwT'], r_const)
    loads_done()
    op('vector', lambda e: e.tensor_copy(out=identb, in_=identf), [r_const], [r_const])
    op('vector', lambda e: e.memset(ones_bf, 1.0), [], [r_const])
    op('vector', lambda e: e.memset(onesS_bf, 1.0 / 512.0), [], [r_const])
    op('vector', lambda e: e.memset(ones_f, 1.0), [], [r_const])
    op('vector', lambda e: e.memset(mhalf, -0.5), [], [r_const])

    def rsqrt_inplace(ap, res):
        p, n = ap.shape[0], ap.shape[1]
        P.op('gpsimd', lambda e: e.tensor_tensor(out=ap, in0=ap, in1=mhalf[0:p, 0:n], op=ALU.pow), [res, r_const], [res])

    w_in_v = D['w_in'].rearrange("(k p) n -> p k n", p=128)

    def load_w(dst, src, res, nsplit=1):
        n = dst.shape[1]
        step = n // nsplit
        g = LoadGroup()
        for i in range(nsplit):
            sl = slice(i * step, (i + 1) * step)
            g.add('gpsimd', dst[:, sl, :], src[:, sl, :], res)
        g.done()

    def phase_A(wkv, r_wkv, load_wB, load_wkv):
        NXS = 6
        xst = [A.f32(1024) for _ in range(NXS)]
        r_xst = [Res() for _ in range(NXS)]
        d_xst = [P.dma_sem() for _ in range(NXS)]
        xsb = [A.bf16(1024) for _ in range(2)]
        r_xsb = [Res() for _ in range(2)]
        junk = A.bf16(1024)
        r_junk = Res()
        nwb = A.f32(1024)
        mnwb = A.f32(1024)
        ss = A.f32(32)
        rstd = A.f32(32)
        memhT = A.bf16(8 * 256).rearrange("p (k t) -> p k t", k=8)
        kvout = [A.f32(512) for _ in range(2)]
        r_kvout = [Res() for _ in range(2)]
        d_kvout = [P.dma_sem() for _ in range(2)]
        r_nw, r_memhT = Res(), Res()
        r_ss = [Res() for _ in range(32)]
        load('sync', nwb, D['norm_w'].partition_broadcast(128), r_nw)
        load('sync', mnwb, D['mem_norm_w'].partition_broadcast(128), r_nw)
        loads_done()

        tiles = []
        for t in range(2):
            tiles.append((D['memp'][t * 128:(t + 1) * 128, :], 128, mnwb, 'mem', t))
        for t in range(16):
            tiles.append((D['xp'][t * 128:(t + 1) * 128, :], 128, nwb, 'x', t))
        tiles.append((D['xs'], 16, nwb, 'x', 16))

        def s1(i):
            src, nr, nt, kind, t = tiles[i]
            s = i % NXS
            xt = xst[s][0:nr]
            ssi, rsi = ss[0:nr, i:i + 1], rstd[0:nr, i:i + 1]
            op('scalar', lambda e: e.activation(out=junk[0:nr], in_=xt, func=AF.Square, accum_out=ssi), [r_xst[s]], [r_ss[i], r_junk])
            op('gpsimd', lambda e: e.tensor_scalar(out=rsi, in0=ssi, scalar1=1.0 / 1024, scalar2=EPS, op0=ALU.mult, op1=ALU.add),
               [r_ss[i]], [r_ss[i]])
            rsqrt_inplace(rsi, r_ss[i])

        def s2(i):
            src, nr, nt, kind, t = tiles[i]
            s, b = i % NXS, i % 2
            xt = xst[s][0:nr]
            rsi = rstd[0:nr, i:i + 1]
            xb = xsb[b][0:nr]
            op('vector', lambda e: e.scalar_tensor_tensor(out=xb, in0=xt, scalar=rsi, in1=nt[0:nr], op0=ALU.mult, op1=ALU.mult),
               [r_xst[s], r_ss[i], r_nw], [r_xsb[b]])
            bk = i % 2
            for kc in range(8):
                tr(bankb[bk][:, kc * nr:(kc + 1) * nr], xb[:, kc * 128:(kc + 1) * 128], identb[0:nr, 0:nr],
                   [r_xsb[b], r_const], [r_bank[bk]], kc == 7)
            src_v = bankb[bk][:, 0:8 * nr].rearrange("p (k t) -> p k t", k=8)
            if kind == 'mem':
                dst, rr = memhT[:, :, t * 128:(t + 1) * 128], r_memhT
            elif t < 16:
                dst, rr = hT[:, :, t * 128:(t + 1) * 128], r_hT[t]
            else:
                dst, rr = hT[:, :, 2048:2064], r_hT[16]
            op('scalar', lambda e: e.copy(out=dst, in_=src_v), [r_bank[bk]], [rr])

        def memkv():
            for mt in range(2):
                for half in range(2):
                    bk2 = 2 + (mt * 2 + half) % 2
                    for kc in range(8):
                        mm(banks[bk2], memhT[:, kc, mt * 128:(mt + 1) * 128], wkv[:, kc, half * 512:(half + 1) * 512],
                           kc == 0, kc == 7, [r_memhT, r_wkv], [r_bank[bk2]], kc == 7)
                    ko = kvout[half]
                    op('scalar', lambda e, ko=ko, bk2=bk2: e.copy(out=ko, in_=banks[bk2]), [r_bank[bk2]], [r_kvout[half]])
                    if half == 1:
                        op('scalar', lambda e, mt=mt, bk2=bk2: e.activation(out=vmem[:, mt, :], in_=banks[bk2], func=AF.Copy),
                           [r_bank[bk2]], [r_kv])
                    dst_d = (D['mkp'] if half == 0 else D['mvp'])[mt * 128:(mt + 1) * 128, :]
                    store('gpsimd', dst_d, ko, [r_kvout[half]], d_kvout[half])
            for h in range(4):
                bk2 = 4 + h % 2
                for kc in range(8):
                    mm(banks[bk2][:, 0:256], wkv[:, kc, h * 128:(h + 1) * 128], memhT[:, kc, :],
                       kc == 0, kc == 7, [r_memhT, r_wkv], [r_bank[bk2]], kc == 7)
                op('scalar', lambda e, h=h, bk2=bk2: e.mul(out=kT[:, h, :], in_=banks[bk2][:, 0:256], mul=128.0 ** -0.5),
                   [r_bank[bk2]], [r_kv])

        def ld(i):
            src, nr, nt, kind, t = tiles[i]
            s = i % NXS
            xt = xst[s][0:nr]
            P.dma('sync', d_xst[s], lambda e: e.dma_start(out=xt, in_=src), [], [r_xst[s]])

        n = len(tiles)
        for i in range(min(NXS - 1, n)):
            ld(i)
        s1(0)
        s1(1)
        P._deps('gpsimd', [r_xst[2]], [])
        load_wkv()
        for i in range(n):
            if i + NXS - 1 < n:
                ld(i + NXS - 1)
            if i + 2 < n:
                s1(i + 2)
            s2(i)
            if i == 10:
                memkv()
            if i == 10:
                load_wB()

    def phase_B(wBv, r_wB, CV, late_loads):
        diag, r_diag, cwT, cwTh, cbT, lnwT, lnbT, r_cv = CV
        cosP = A.f32(16 * 64).rearrange("p (t f) -> p t f", t=16)
        sinP = A.f32(16 * 128).rearrange("p (t a f) -> p t a f", t=16, a=2)
        gk, epsT, cmask, ctab = A.f32(4), A.f32(4), A.f32(128), A.f32(512)
        r_tab = Res()
        r_rot = [Res() for _ in range(16)]
        load('sync', gk, D['gk'], r_tab)
        load('sync', epsT, D['epsT'], r_tab)
        load('sync', cmask, D['cmask'], r_tab)
        load('sync', ctab, D['ctab'], r_tab)
        load('sync', cosP[:, 0:2, :], D['cosP'][:, 0:2, :], r_rot[0])
        load('sync', sinP[:, 0:2, :, :], D['sinP'][:, 0:2, :, :], r_rot[1])
        loads_done()
        r_rot[0].w = r_rot[1].w
        load('sync', cosP[:, 2:16, :], D['cosP'][:, 2:16, :], r_rot[2])
        load('sync', sinP[:, 2:16, :, :], D['sinP'][:, 2:16, :, :], r_rot[3])
        load('sync', cwT, D['cwT'], r_cv)
        load('sync', cbT, D['cbT'], r_cv)
        load('sync', lnwT, D['lnwT'], r_cv)
        load('sync', lnbT, D['lnbT'], r_cv)
        load('sync', cosS_p[0:16], D['cosS'], r_rotS)
        load('sync', sinS_p[0:16], D['sinS'], r_rotS)
        loads_done()
        for i in range(4, 16):
            r_rot[i] = r_rot[3]
        r_rot[2].w = r_rot[3].w
        zq_sb, Bq, zk_sb, Bk = A.f32(512), A.f32(512), A.f32(512), A.f32(512)
        r_zq, r_zk = Res(), Res()
        r_Bq, r_Bk = [Res(), Res()], [Res(), Res()]
        qb = [A.bf16(512) for _ in range(2)]
        kb = [A.bf16(512) for _ in range(2)]
        vtb = [A.bf16(512) for _ in range(3)]
        r_vtb3 = [Res() for _ in range(3)]
        sg = [A.bf16(512) for _ in range(4)]
        r_sg3 = [Res() for _ in range(4)]
        nm4 = A.f32(4)
        negm4 = A.f32(4)
        r_nm = Res()
        qkT = [A.bf16(1024) for _ in range(2)]
        scT = [A.bf16(512) for _ in range(2)]
        prer = [A.bf16(512) for _ in range(2)]
        r_qb, r_kb, r_vtb, r_sg, r_qkT, r_scT, r_prer = [[Res(), Res()] for _ in range(7)]
        retn = A.f32(512)
        r_retn = [Res() for _ in range(4)]
        s_f, tmp = A.f32(512), A.f32(512)
        s_bf = A.bf16(512)
        r_s, r_tmp, r_sbf = Res(), Res(), Res()
        bnst, mv, rs4 = A.f32(24), A.f32(8), A.f32(4)
        r_st = Res()
        r_bn4 = [Res() for _ in range(4)]
        r_mv4 = [Res() for _ in range(4)]
        op('vector', lambda e: e.memset(s_f, 0.0), [], [r_s])
        op('vector', lambda e: e.memset(s_bf, 0.0), [], [r_sbf])
        ZQ, ZK, ZV, ZG, PT, SK, OO0, OO1 = range(8)
        OOB = [OO0, OO1]
        SC = KS = SK
        OO = OO0

        def v4(ap):
            return ap.rearrange("p (h a f) -> p h a f", h=4, a=2)

        def rotary(zsb, r_z, Bx, r_B, t, outb, r_out):
            zb = v4(zsb)
            for a in range(2):
                sn = sinP[:, t, a, :].unsqueeze(1).to_broadcast([128, 4, 64])
                op('vector', lambda e, a=a, sn=sn: e.tensor_tensor(out=v4(Bx)[:, :, a, :], in0=zb[:, :, 1 - a, :], in1=sn, op=ALU.mult),
                   [r_z, r_rot[t]], [r_B[a]])
            cs = cosP[:, t, :].unsqueeze(1).unsqueeze(1).to_broadcast([128, 4, 2, 64])
            op('vector', lambda e: e.tensor_tensor(out=zb, in0=zb, in1=cs, op=ALU.mult), [r_z, r_rot[t]] + r_B, [r_z])
            op('gpsimd', lambda e: e.tensor_tensor(out=outb, in0=zsb, in1=Bx, op=ALU.add), [r_z] + r_B, [r_out])

        ZBANK = {3: ZG, 2: ZV, 1: ZK, 0: ZQ}

        def z_blk(t, blk):
            bank = ZBANK[blk]
            for kc in range(8):
                mm(banks[bank], hT[:, kc, t * 128:(t + 1) * 128], wBv[:, kc, blk * 512:(blk + 1) * 512],
                   kc == 0, kc == 7, [r_hT[t], r_wB], [r_bank[bank]], kc == 7)

        def ev_g(t):
            op('scalar', lambda e: e.activation(out=sg[t % 4], in_=banks[ZG], func=AF.Silu), [r_bank[ZG]], [r_sg3[t % 4]])

        def ev_v(t):
            gkb = gk.unsqueeze(2).to_broadcast([128, 4, 128])
            op('vector', lambda e: e.tensor_tensor(out=vtb[t % 3].rearrange("p (h f) -> p h f", h=4),
                                                   in0=banks[ZV].rearrange("p (h f) -> p h f", h=4), in1=gkb, op=ALU.mult),
               [r_bank[ZV], r_tab], [r_vtb3[t % 3]])

        def ev_kq(t):
            op('scalar', lambda e: e.copy(out=zk_sb, in_=banks[ZK]), [r_bank[ZK]], [r_zk])
            op('scalar', lambda e: e.copy(out=zq_sb, in_=banks[ZQ]), [r_bank[ZQ]], [r_zq])

        def rot_k(t):
            rotary(zk_sb, r_zk, Bk, r_Bk, t, kb[t % 2], r_kb[t % 2])

        def rot_q(t):
            rotary(zq_sb, r_zq, Bq, r_Bq, t, qb[t % 2], r_qb[t % 2])

        def tail1a(t):
            b = t % 2
            for h in range(4):
                tr(bankb[PT][:, h * 128:(h + 1) * 128], qb[b][:, h * 128:(h + 1) * 128], identb, [r_qb[b], r_const], [r_bank[PT]], False)
            for h in range(4):
                tr(bankb[PT][:, (4 + h) * 128:(5 + h) * 128], kb[b][:, h * 128:(h + 1) * 128], identb, [r_kb[b], r_const], [r_bank[PT]], h == 3)
            op('scalar', lambda e: e.copy(out=qkT[b], in_=bankb[PT]), [r_bank[PT]], [r_qkT[b]])

        def tail1b_sc(t):
            b = t % 2
            for h in range(4):
                mm(banks[SK][:, h * 128:(h + 1) * 128], qkT[b][:, (4 + h) * 128:(5 + h) * 128], qkT[b][:, h * 128:(h + 1) * 128],
                   True, True, [r_qkT[b]], [r_bank[SK]], h == 3)
            cmb = cmask.unsqueeze(1).to_broadcast([128, 4, 128])
            op('vector', lambda e: e.tensor_tensor(out=scT[b].rearrange("p (h f) -> p h f", h=4),
                                                   in0=banks[SK].rearrange("p (h f) -> p h f", h=4), in1=cmb, op=ALU.mult),
               [r_bank[SK], r_tab], [r_scT[b]])

        def tail1b_os(t):
            b = t % 2
            v3 = t % 3
            ob = OOB[t % 2]
            for h in range(4):
                sl = slice(h * 128, (h + 1) * 128)
                mm(banks[ob][:, sl], scT[b][:, sl], vtb[v3][:, sl], True, False, [r_scT[b], r_vtb3[v3]], [r_bank[ob]], False)
                mm(banks[ob][:, sl], qkT[b][:, sl], s_bf[:, sl], False, True, [r_qkT[b], r_sbf], [r_bank[ob]], h == 3)
            for h in range(4):
                sl = slice(h * 128, (h + 1) * 128)
                mm(banks[SK][:, sl], kb[b][:, sl], vtb[v3][:, sl], True, True, [r_kb[b], r_vtb3[v3]], [r_bank[SK]], h == 3)
            op('vector', lambda e: e.tensor_tensor(out=tmp, in0=banks[SK], in1=s_f, op=ALU.add), [r_bank[SK], r_s], [r_tmp])
            op('vector', lambda e: e.tensor_tensor(out=s_bf, in0=tmp, in1=ctab, op=ALU.mult), [r_tmp, r_tab], [r_sbf])
            op('gpsimd', lambda e: e.tensor_tensor(out=s_f, in0=tmp, in1=ctab, op=ALU.mult), [r_tmp, r_tab], [r_s])

        def tail2a(t):
            b = t % 2
            ob = OOB[t % 2]
            for h in range(4):
                op('vector', lambda e, h=h: e.bn_stats(out=bnst[:, h * 6:(h + 1) * 6], in_=banks[ob][:, h * 128:(h + 1) * 128]),
                   [r_bank[ob]], [r_bn4[h]])
            for h in range(4):
                op('vector', lambda e, h=h: e.bn_aggr(out=mv[:, h * 2:(h + 1) * 2], in_=bnst[:, h * 6:(h + 1) * 6]), [r_bn4[h]], [r_mv4[h]])
            mvv = mv.rearrange("p (h a) -> p h a", a=2)
            op('vector', lambda e: e.tensor_tensor(out=rs4, in0=mvv[:, :, 1], in1=epsT, op=ALU.add), r_mv4 + [r_tab], [r_st])
            op('vector', lambda e: e.tensor_scalar(out=negm4, in0=mvv[:, :, 0], scalar1=-1.0, scalar2=None, op0=ALU.mult), r_mv4, [r_nm])
            rsqrt_inplace(rs4, r_st)
            op('gpsimd', lambda e: e.tensor_tensor(out=nm4, in0=negm4, in1=rs4, op=ALU.mult), [r_st, r_nm], [r_nm])

        def tail2a2(t):
            b = t % 2
            ob = OOB[t % 2]
            for h in range(4):
                sl = slice(h * 128, (h + 1) * 128)
                op('scalar', lambda e, h=h, sl=sl: e.activation(out=retn[:, sl], in_=banks[ob][:, sl], func=AF.Identity,
                                                               scale=rs4[:, h:h + 1], bias=nm4[:, h:h + 1]),
                   [r_bank[ob], r_st, r_nm], [r_retn[h]])
            op('gpsimd', lambda e: e.tensor_tensor(out=prer[b], in0=retn, in1=sg[t % 4], op=ALU.mult), r_retn + [r_sg3[t % 4]], [r_prer[b]])

        def tail2b(t):
            b = t % 2
            for h in range(4):
                tr(bankb[PT][:, h * 128:(h + 1) * 128], prer[b][:, h * 128:(h + 1) * 128], identb, [r_prer[b], r_const], [r_bank[PT]], h == 3)
            op('scalar', lambda e: e.copy(out=preT[0][:, :, t * 128:(t + 1) * 128], in_=bankb[PT][:, 0:512].rearrange("p (h t) -> p h t", h=4)),
               [r_bank[PT]], [r_preT[0][t // 4]])

        def ok(i):
            return 0 <= i < 16

        def zs_mms():
            for blk, bank in ((3, ZG), (2, ZV), (1, ZK), (0, ZQ)):
                for kc in range(8):
                    mm(banks[bank][0:16, :], hT[:, kc, 2048:2064], wBv[:, kc, blk * 512:(blk + 1) * 512],
                       kc == 0, kc == 7, [r_hT[16], r_wB], [r_bank[bank]], kc == 7)

        for t in range(-1, 19):
            if t == 2:
                late_loads()
            if ok(t):
                z_blk(t, 1)
                z_blk(t, 0)
                ev_kq(t)
            if t == 16:
                zs_mms()
            if ok(t - 1):
                tail1a(t - 1)
            if ok(t + 1):
                z_blk(t + 1, 3)
                ev_g(t + 1)
            if ok(t - 3):
                tail2b(t - 3)
            if ok(t - 2):
                tail2a(t - 2)
            if ok(t - 1):
                tail1b_sc(t - 1)
            if ok(t):
                rot_k(t)
            if ok(t + 1):
                z_blk(t + 1, 2)
                ev_v(t + 1)
            if ok(t - 2):
                tail2a2(t - 2)
            if ok(t):
                rot_q(t)
            if ok(t - 1):
                tail1b_os(t - 1)
        store('sync', D['srp'].rearrange("h d e -> d h e"), s_f.rearrange("p (h e) -> p h e", h=4), [r_s])

        P.barrier()
        A.off = B_LOCAL
        GAM = [1.0 - 2.0 ** (-5.0 - h) for h in range(4)]
        cosS, sinS = cosS_p, sinS_p
        gtab16 = A.f32(512)
        delta16 = A.f32(256).rearrange("p (a b) -> p a b", a=16)
        r_t2 = Res()
        load('sync', gtab16[0:16], D['gtab16'], r_t2)
        load('sync', delta16, D['delta16'], r_t2)
        loads_done()
        q_s, k_s, v_s, sg_s, As, Bs = [A.f32(512) for _ in range(6)]
        v_sb = A.bf16(512)
        r_q, r_k, r_v, r_sgs, r_As = Res(), Res(), Res(), Res(), Res()
        qk4 = A.f32(4)
        qTs = A.f32(64).rearrange("p (h b) -> p h b", h=4)
        Qm = A.bf16(1024).rearrange("p (h b c) -> p h b c", h=4, b=16)
        r_qTs, r_Qm = Res(), Res()
        Sf = [A.f32(512).rearrange("p (h e) -> p h e", h=4) for _ in range(4)]
        r_Sf = [Res() for _ in range(4)]
        d_Sf = [P.dma_sem() for _ in range(4)]
        Sbf = [A.bf16(512) for _ in range(2)]
        r_Sbf = [Res(), Res()]
        Km = [A.bf16(512) for _ in range(2)]
        r_Km = [Res(), Res()]
        snew = [A.f32(512) for _ in range(3)]
        r_snew = [[Res() for _ in range(4)] for _ in range(3)]
        d_snew = [P.dma_sem() for _ in range(3)]
        o_s, pr_s = A.f32(512), A.f32(512)
        r_os = Res()
        bn2, mv2, rs2 = A.f32(24), A.f32(8), A.f32(4)
        def load_S(b):
            s4 = b % 4
            src = D['sret'][b].rearrange("h d e -> d h e")
            P.dma('sync', d_Sf[s4], lambda e: e.dma_start(out=Sf[s4], in_=src), [], [r_Sf[s4]])
        for b in range(3):
            load_S(b)
        op('vector', lambda e: e.tensor_scalar(out=cwTh, in0=cwT, scalar1=0.5, scalar2=None, op0=ALU.mult), [r_cv], [r_cv])
        P._deps('vector', [], [r_wB])
        P._deps('scalar', [], [r_wB])
        diag_ops = []
        idx = 0
        for k in range(31):
            for cc in range(4):
                if idx % 2 == 0:
                    diag_ops.append(('vector', lambda e, k=k, cc=cc: e.tensor_scalar(out=diag[:, k * 4 + cc, :], in0=identf, scalar1=cwTh[:, cc, k:k + 1],
                                                                                    scalar2=None, op0=ALU.mult), [r_cv, r_const], [r_diag[k * 4 + cc]]))
                else:
                    diag_ops.append(('scalar', lambda e, k=k, cc=cc: e.activation(out=diag[:, k * 4 + cc, :], in_=identf, func=AF.Copy,
                                                                                 scale=cwTh[:, cc, k:k + 1]), [r_cv, r_const], [r_diag[k * 4 + cc]]))
                idx += 1
        op('scalar', lambda e: e.activation(out=sg_s[0:16], in_=banks[ZG][0:16], func=AF.Silu), [r_bank[ZG]], [r_sgs])
        op('scalar', lambda e: e.copy(out=v_s[0:16], in_=banks[ZV][0:16]), [r_bank[ZV]], [r_v])
        op('scalar', lambda e: e.activation(out=v_sb[0:16], in_=banks[ZV][0:16], func=AF.Copy), [r_bank[ZV]], [r_v])

        def v4s(ap):
            return ap[0:16].rearrange("p (h a f) -> p h a f", h=4, a=2)

        for bank, dst, rr, scale in ((ZK, k_s, r_k, 1.0), (ZQ, q_s, r_q, 128.0 ** -0.5)):
            cs = cosS[0:16].unsqueeze(1).unsqueeze(1).to_broadcast([16, 4, 2, 64])
            zb = v4s(banks[bank])
            op('vector', lambda e, zb=zb, cs=cs: e.tensor_tensor(out=v4s(As), in0=zb, in1=cs, op=ALU.mult), [r_bank[bank], r_rotS], [r_As])
            for a in range(2):
                sn = sinS[0:16, a, :].unsqueeze(1).to_broadcast([16, 4, 64])
                op('vector', lambda e, a=a, sn=sn, zb=zb: e.tensor_tensor(out=v4s(Bs)[:, :, a, :], in0=zb[:, :, 1 - a, :], in1=sn, op=ALU.mult),
                   [r_bank[bank], r_rotS], [r_As])
            op('vector', lambda e, dst=dst: e.tensor_tensor(out=dst[0:16], in0=As[0:16], in1=Bs[0:16], op=ALU.add), [r_As], [rr])
            if scale != 1.0:
                op('vector', lambda e, dst=dst, scale=scale: e.tensor_scalar(out=dst[0:16], in0=dst[0:16], scalar1=scale, scalar2=None, op0=ALU.mult),
                   [rr], [rr])
        op('vector', lambda e: e.tensor_tensor(out=As[0:16], in0=q_s[0:16], in1=k_s[0:16], op=ALU.mult), [r_q, r_k, r_As], [r_As])
        op('vector', lambda e: e.tensor_reduce(out=qk4[0:16], in_=As[0:16].rearrange("p (h f) -> p h f", h=4), axis=AX.X, op=ALU.add),
           [r_As], [r_As])
        for h in range(4):
            tr(banks[PT][:, h * 16:(h + 1) * 16], q_s[0:16, h * 128:(h + 1) * 128], identf[0:16, 0:16], [r_q, r_const], [r_bank[PT]], h == 3)
        op('vector', lambda e: e.tensor_copy(out=qTs, in_=banks[PT][:, 0:64].rearrange("p (h b) -> p h b", h=4)), [r_bank[PT]], [r_qTs])
        op('vector', lambda e: e.tensor_tensor(out=Qm, in0=qTs.unsqueeze(3).to_broadcast([128, 4, 16, 16]),
                                               in1=delta16.unsqueeze(1).to_broadcast([128, 4, 16, 16]), op=ALU.mult),
           [r_qTs, r_t2], [r_Qm])
        OB = [SK, OO0, OO1, ZG]
        def prep(b):
            s4, s2 = b % 4, b % 2
            op('scalar', lambda e: e.activation(out=Sbf[s2], in_=Sf[s4].rearrange("p h e -> p (h e)"), func=AF.Copy),
               [r_Sf[s4]], [r_Sbf[s2]])
            op('vector', lambda e: e.tensor_scalar(out=Km[s2][0:16], in0=k_s[0:16], scalar1=identf[0:16, b:b + 1], scalar2=None,
                                                   op0=ALU.mult), [r_k, r_const], [r_Km[s2]])

        prep(0)
        for b in range(16):
            s4, s2, s3 = b % 4, b % 2, b % 3
            if b + 3 < 16:
                load_S(b + 3)
            if b + 1 < 16:
                prep(b + 1)
            for h in range(4):
                mm(banks[OB[h]][0:16, 0:128], Qm[:, h, b, :], Sbf[s2][:, h * 128:(h + 1) * 128], b == 0, b == 15, [r_Qm, r_Sbf[s2]], [r_bank[OB[h]]], True)
            bkS = ZV if b % 2 == 0 else ZK
            for h in range(4):
                sl = slice(h * 128, (h + 1) * 128)
                mm(banks[bkS][:, sl], Km[s2][0:16, sl], v_sb[0:16, sl], True, True, [r_Km[s2], r_v], [r_bank[bkS]], h == 3)
            for h in range(4):
                sl = slice(h * 128, (h + 1) * 128)
                op('vector', lambda e, h=h, sl=sl, s4=s4, s3=s3, bkS=bkS: e.scalar_tensor_tensor(
                    out=snew[s3][:, sl], in0=Sf[s4][:, h, :], scalar=GAM[h], in1=banks[bkS][:, sl], op0=ALU.mult, op1=ALU.add),
                   [r_Sf[s4], r_bank[bkS]], [r_snew[s3][h]])
            store('gpsimd', D['srs'][b].rearrange("h d e -> d h e"), snew[s3].rearrange("p (h e) -> p h e", h=4), r_snew[s3], d_snew[s3])
            for _ in range(8):
                if diag_ops:
                    op(*diag_ops.pop(0))
        while diag_ops:
            op(*diag_ops.pop(0))
        for h in range(4):
            sl = slice(h * 128, (h + 1) * 128)
            op('vector', lambda e, h=h, sl=sl: e.tensor_tensor(out=o_s[0:16, sl], in0=banks[OB[h]][0:16, 0:128], in1=gtab16[0:16, sl], op=ALU.mult),
               [r_bank[OB[h]], r_t2], [r_os])
            op('vector', lambda e, h=h, sl=sl: e.scalar_tensor_tensor(out=o_s[0:16, sl], in0=v_s[0:16, sl], scalar=qk4[0:16, h:h + 1],
                                                                      in1=o_s[0:16, sl], op0=ALU.mult, op1=ALU.add), [r_v, r_As, r_os], [r_os])
        for h in range(4):
            op('vector', lambda e, h=h: e.bn_stats(out=bn2[0:16, h * 6:(h + 1) * 6], in_=o_s[0:16, h * 128:(h + 1) * 128]), [r_os], [r_os])
        for h in range(4):
            op('vector', lambda e, h=h: e.bn_aggr(out=mv2[0:16, h * 2:(h + 1) * 2], in_=bn2[0:16, h * 6:(h + 1) * 6]), [r_os], [r_os])
        op('vector', lambda e: e.tensor_scalar(out=rs2[0:16], in0=mv2[0:16].rearrange("p (h a) -> p h a", a=2)[:, :, 1], scalar1=EPS, scalar2=None,
                                               op0=ALU.add), [r_os], [r_os])
        rsqrt_inplace(rs2[0:16], r_os)
        for h in range(4):
            sl = slice(h * 128, (h + 1) * 128)
            op('vector', lambda e, h=h, sl=sl: e.tensor_scalar(out=pr_s[0:16, sl], in0=o_s[0:16, sl], scalar1=mv2[0:16, 2 * h:2 * h + 1],
                                                             scalar2=rs2[0:16, h:h + 1], op0=ALU.subtract, op1=ALU.mult), [r_os], [r_os])
        op('vector', lambda e: e.tensor_tensor(out=pr_s[0:16], in0=pr_s[0:16], in1=sg_s[0:16], op=ALU.mult), [r_os, r_sgs], [r_os])
        for h in range(4):
            tr(banks[PT][:, 64 + h * 16:64 + (h + 1) * 16], pr_s[0:16, h * 128:(h + 1) * 128], identf[0:16, 0:16], [r_os, r_const], [r_bank[PT]], h == 3)
        for h in range(4):
            op('scalar', lambda e, h=h: e.activation(out=preT[0][:, h, 2048:2064], in_=banks[PT][:, 64 + h * 16:64 + (h + 1) * 16],
                                                     func=AF.Copy, scale=gnwT[:, h:h + 1]), [r_bank[PT], r_const], [r_preT[0][4]])

    def phase_C(wCv, r_wC, CV, wb_spare, diag_ar):
        nonlocal A
        diag, r_diag, cwT, cwTh, cbT, lnwT, lnbT, r_cv = CV
        UW = 2080
        uT = A.bf16(4 * UW).rearrange("p (c t) -> p c t", c=4)
        r_uT = [Res() for _ in range(4)]
        r_uz = Res()
        op('vector', lambda e: e.memset(uT[:, :, 0:32], 0.0), [], [r_uz])
        y = A.f32(2048).rearrange("p (c t) -> p c t", c=4)
        ybf = A.bf16(2048).rearrange("p (c t) -> p c t", c=4)
        ysq = A.bf16(2048).rearrange("p (c t) -> p c t", c=4)
        r_y = [Res() for _ in range(4)]
        r_ybf = [Res() for _ in range(4)]
        r_ysq = [Res() for _ in range(4)]
        rstd, nmr = A.f32(512), A.f32(512)
        r_stt = Res()
        sgc = [A.bf16(2048).rearrange("p (c t) -> p c t", c=4) for _ in range(2)]
        r_sgc = [[Res() for _ in range(4)] for _ in range(2)]
        tt = [wb_spare.f32(512) for _ in range(2)]
        r_tt = [Res(), Res()]
        yn = [wb_spare.f32(512), A.f32(512)]
        cs_ = [A.f32(512), A.f32(512)]
        r_yn, r_cs = [Res(), Res()], [Res(), Res()]
        nb = [0]

        def nbank():
            nb[0] = (nb[0] + 1) % 8
            return nb[0]

        def zc(j, c0, n):
            bk = nbank()
            for kc in range(8):
                mm(banks[bk][:, 0:n], wCv[:, kc, j * 128:(j + 1) * 128], hT[:, kc, c0:c0 + n], kc == 0, kc == 7,
                   [r_hT[c0 // 128 + i] for i in range(max(1, n // 128))] + [r_wC], [r_bank[bk]], kc == 7)
            return bk

        def zstage(tb):
            c0 = tb * 512
            sp = tb % 2
            for cc in range(4):
                i2 = cc % 2
                bkb = zc(4 + cc, c0, 512)
                op('scalar', lambda e, i2=i2, bkb=bkb: e.activation(out=tt[i2], in_=banks[bkb], func=AF.Tanh, scale=0.5), [r_bank[bkb]], [r_tt[i2]])
                bka = zc(cc, c0, 512)
                op('vector', lambda e, i2=i2, bka=bka, cc=cc, c0=c0: e.scalar_tensor_tensor(
                    out=uT[:, cc, 30 + c0:30 + c0 + 512], in0=tt[i2], scalar=1.0, in1=banks[bka], op0=ALU.add, op1=ALU.mult),
                   [r_tt[i2], r_bank[bka], r_uz], [r_uT[tb]])
                bkg = zc(8 + cc, c0, 512)
                op('scalar', lambda e, cc=cc, bkg=bkg, sp=sp: e.activation(out=sgc[sp][:, cc, :], in_=banks[bkg], func=AF.Silu),
                   [r_bank[bkg]], [r_sgc[sp][cc]])
            if tb == 3:
                bkb = nbank()
                for kc in range(8):
                    mm(banks[bkb], hT[:, kc, 1920:2048], wCv[:, kc, 512:1024], kc == 0, kc == 7, [r_hT[15], r_wC], [r_bank[bkb]], kc == 7)
                op('scalar', lambda e, bkb=bkb: e.activation(out=yn[0], in_=banks[bkb], func=AF.Tanh, scale=0.5), [r_bank[bkb], r_yn[0]], [r_yn[0]])
                bka = nbank()
                for kc in range(8):
                    mm(banks[bka], hT[:, kc, 1920:2048], wCv[:, kc, 0:512], kc == 0, kc == 7, [r_hT[15], r_wC], [r_bank[bka]], kc == 7)
                op('vector', lambda e, bka=bka: e.scalar_tensor_tensor(out=yn[0], in0=yn[0], scalar=1.0, in1=banks[bka], op0=ALU.add, op1=ALU.mult),
                   [r_yn[0], r_bank[bka]], [r_yn[0]])
                op('vector', lambda e: e.tensor_scalar(out=cs_[0], in0=yn[0], scalar1=0.5, scalar2=None, op0=ALU.mult), [r_yn[0], r_cs[0]], [r_cs[0]])
                store('gpsimd', D['scp'], cs_[0][98:128, :], [r_cs[0]], P.dma_sem())

        def conv(tb):
            c0 = tb * 512
            for cc in range(4):
                bk = nbank()
                for k in range(31):
                    mm(banks[bk], diag[:, k * 4 + cc, :], uT[:, cc, c0 + k:c0 + k + 512], k == 0, k == 30,
                       [r_diag[k * 4 + cc], r_uz, r_uT[tb]] + ([r_uT[tb - 1]] if tb > 0 else []), [r_bank[bk]], k == 30)
                op('scalar', lambda e, cc=cc, bk=bk: e.activation(out=y[:, cc, :], in_=banks[bk], func=AF.Identity, bias=cbT[:, cc:cc + 1], scale=1.0),
                   [r_bank[bk], r_cv], [r_y[cc]])
                op('scalar', lambda e, cc=cc, bk=bk: e.activation(out=ybf[:, cc, :], in_=banks[bk], func=AF.Identity, bias=cbT[:, cc:cc + 1], scale=1.0),
                   [r_bank[bk], r_cv], [r_ybf[cc]])
                op('scalar', lambda e, cc=cc, bk=bk: e.activation(out=ysq[:, cc, :], in_=banks[bk], func=AF.Square, bias=cbT[:, cc:cc + 1], scale=1.0),
                   [r_bank[bk], r_cv], [r_ysq[cc]])

        def stats(tb):
            bkm, bkq = nbank(), nbank()
            for cc in range(4):
                mm(banks[bkm], onesS_bf, ybf[:, cc, :], cc == 0, cc == 3, [r_const, r_ybf[cc]], [r_bank[bkm]], cc == 3)
            for cc in range(4):
                mm(banks[bkq], onesS_bf, ysq[:, cc, :], cc == 0, cc == 3, [r_const, r_ysq[cc]], [r_bank[bkq]], cc == 3)
            op('scalar', lambda e: e.activation(out=nmr, in_=banks[bkm], func=AF.Square), [r_bank[bkm], r_stt], [r_stt])
            op('vector', lambda e: e.scalar_tensor_tensor(out=rstd, in0=banks[bkq], scalar=EPS, in1=nmr, op0=ALU.add, op1=ALU.subtract),
               [r_bank[bkq], r_stt], [r_stt])
            op('scalar', lambda e: e.activation(out=rstd, in_=rstd, func=AF.Sqrt), [r_stt], [r_stt])
            op('vector', lambda e: e.reciprocal(out=rstd, in_=rstd), [r_stt], [r_stt])
            op('vector', lambda e: e.scalar_tensor_tensor(out=nmr, in0=banks[bkm], scalar=-1.0, in1=rstd, op0=ALU.mult, op1=ALU.mult),
               [r_bank[bkm], r_stt], [r_stt])

        def norm(tb):
            c0 = tb * 512
            sp = tb % 2
            for cc in range(4):
                i2 = cc % 2
                op('vector', lambda e, cc=cc, i2=i2: e.tensor_tensor(out=yn[i2], in0=y[:, cc, :], in1=rstd, op=ALU.mult), [r_y[cc], r_stt, r_yn[i2]], [r_yn[i2]])
                op('vector', lambda e, i2=i2: e.tensor_tensor(out=yn[i2], in0=yn[i2], in1=nmr, op=ALU.add), [r_yn[i2], r_stt], [r_yn[i2]])
                op('scalar', lambda e, cc=cc, i2=i2: e.activation(out=cs_[i2], in_=yn[i2], func=AF.Silu, bias=lnbT[:, cc:cc + 1], scale=lnwT[:, cc:cc + 1]),
                   [r_yn[i2], r_cv, r_cs[i2]], [r_cs[i2]])
                op('gpsimd', lambda e, cc=cc, c0=c0, i2=i2, sp=sp: e.tensor_tensor(out=preT[1][:, cc, c0:c0 + 512], in0=cs_[i2], in1=sgc[sp][:, cc, :], op=ALU.mult),
                   [r_cs[i2], r_sgc[sp][cc]], [r_preT[1][tb]])

        zstage(0)
        for tb in range(4):
            if tb + 1 < 4:
                zstage(tb + 1)
            conv(tb)
            stats(tb)
            norm(tb)
        load_wDj()

        A2 = Arena(diag_ar.ap, 8192)
        for eng_ in ('sync', 'vector', 'scalar', 'gpsimd'):
            P._deps(eng_, [], r_diag)
        A_keep = A
        A = A2
        wrep, w30b, cbb, lnwb, lnbb = [A.f32(512) for _ in range(5)]
        ind = A.f32(64).rearrange("p (i b) -> p i b", i=4)
        r_c2 = Res()
        load('sync', wrep[0:120], D['wrep'], r_c2)
        load('sync', w30b[0:16], D['w30b'], r_c2)
        load('sync', cbb[0:16], D['cbb'], r_c2)
        load('sync', lnwb[0:16], D['lnwb'], r_c2)
        load('sync', lnbb[0:16], D['lnbb'], r_c2)
        load('sync', ind[0:120], D['ind'], r_c2)
        loads_done()
        stt_ = [A.f32(512) for _ in range(4)]
        r_stt2 = [Res() for _ in range(4)]
        u_s, t1, cv, sgcs = [A.f32(512) for _ in range(4)]
        r_us, r_t1, r_cvs, r_sg2 = Res(), Res(), Res(), Res()
        bn3, mv3, rs3 = A.f32(8), A.f32(4), A.f32(2)
        sview = D['sconv'].rearrange("b k c -> (b k) c")
        for i in range(4):
            load('sync', stt_[i][0:120], sview[i * 120:(i + 1) * 120, :], r_stt2[i])
        loads_done()
        store('gpsimd', D['scs'][:, 0:29, :], D['sconv'][:, 1:30, :], [])
        bz = []
        for j in range(3):
            bk = nbank()
            for kc in range(8):
                mm(banks[bk][0:16, :], hT[:, kc, 2048:2064], wCv[:, kc, j * 512:(j + 1) * 512], kc == 0, kc == 7, [r_hT[16], r_wC], [r_bank[bk]], kc == 7)
            bz.append(bk)
        op('scalar', lambda e: e.activation(out=t1[0:16], in_=banks[bz[1]][0:16], func=AF.Tanh, scale=0.5), [r_bank[bz[1]]], [r_t1])
        op('vector', lambda e: e.scalar_tensor_tensor(out=t1[0:16], in0=t1[0:16], scalar=1.0, in1=banks[bz[0]][0:16], op0=ALU.add, op1=ALU.mult),
           [r_t1, r_bank[bz[0]]], [r_t1])
        op('vector', lambda e: e.tensor_scalar(out=u_s[0:16], in0=t1[0:16], scalar1=0.5, scalar2=None, op0=ALU.mult), [r_t1], [r_us])
        op('scalar', lambda e: e.activation(out=sgcs[0:16], in_=banks[bz[2]][0:16], func=AF.Silu), [r_bank[bz[2]]], [r_sg2])
        store('gpsimd', D['scs'][:, 29, :], u_s[0:16], [r_us])
        for i in range(4):
            op('vector', lambda e, i=i: e.tensor_tensor(out=stt_[i][0:120], in0=stt_[i][0:120], in1=wrep[0:120], op=ALU.mult), [r_stt2[i], r_c2], [r_stt2[i]])
        bk = nbank()
        for i in range(4):
            mm(banks[bk][0:16, :], ind[0:120, i, :], stt_[i][0:120], i == 0, i == 3, [r_c2, r_stt2[i]], [r_bank[bk]], i == 3)
        op('vector', lambda e: e.tensor_tensor(out=t1[0:16], in0=u_s[0:16], in1=w30b[0:16], op=ALU.mult), [r_us, r_c2, r_t1], [r_t1])
        op('vector', lambda e: e.tensor_tensor(out=t1[0:16], in0=t1[0:16], in1=cbb[0:16], op=ALU.add), [r_t1, r_c2], [r_t1])
        op('vector', lambda e, bk=bk: e.tensor_tensor(out=cv[0:16], in0=banks[bk][0:16], in1=t1[0:16], op=ALU.add), [r_bank[bk], r_t1], [r_cvs])
        op('vector', lambda e: e.bn_stats(out=bn3[0:16, 0:6], in_=cv[0:16]), [r_cvs], [r_cvs])
        op('vector', lambda e: e.bn_aggr(out=mv3[0:16, 0:2], in_=bn3[0:16, 0:6]), [r_cvs], [r_cvs])
        op('vector', lambda e: e.tensor_scalar(out=rs3[0:16, 0:1], in0=mv3[0:16, 1:2], scalar1=EPS, scalar2=None, op0=ALU.add), [r_cvs], [r_cvs])
        rsqrt_inplace(rs3[0:16, 0:1], r_cvs)
        op('vector', lambda e: e.tensor_scalar(out=cv[0:16], in0=cv[0:16], scalar1=mv3[0:16, 0:1], scalar2=rs3[0:16, 0:1], op0=ALU.subtract, op1=ALU.mult),
           [r_cvs], [r_cvs])
        op('vector', lambda e: e.tensor_tensor(out=cv[0:16], in0=cv[0:16], in1=lnwb[0:16], op=ALU.mult), [r_cvs, r_c2], [r_cvs])
        op('vector', lambda e: e.tensor_tensor(out=cv[0:16], in0=cv[0:16], in1=lnbb[0:16], op=ALU.add), [r_cvs, r_c2], [r_cvs])
        op('scalar', lambda e: e.activation(out=t1[0:16], in_=cv[0:16], func=AF.Silu), [r_cvs, r_t1], [r_t1])
        op('vector', lambda e: e.tensor_tensor(out=t1[0:16], in0=t1[0:16], in1=sgcs[0:16], op=ALU.mult), [r_t1, r_sg2], [r_t1])
        bk = nbank()
        for h in range(4):
            tr(banks[bk][:, h * 16:(h + 1) * 16], t1[0:16, h * 128:(h + 1) * 128], identf[0:16, 0:16], [r_t1, r_const], [r_bank[bk]], h == 3)
        op('scalar', lambda e, bk=bk: e.copy(out=preT[1][:, :, 2048:2064], in_=banks[bk][:, 0:64].rearrange("p (h b) -> p h b", h=4)),
           [r_bank[bk]], [r_preT[1][4]])
        A = A_keep

    def phase_D(wDj, r_wDj, qmT):
        qm_s, sgm_s = A.f32(512), A.f32(512)
        r_qms, r_sgms = Res(), Res()
        assert A.off == WDJ_OFF
        A.f32(4096)
        D2_LOCAL = A.off
        Kslot = [None] * 16
        r_K = [Res() for _ in range(16)]
        d_K = [P.dma_sem() for _ in range(16)]
        for i in range(6):
            Kslot[i] = tbuf[i // 3][i % 3].bitcast(BF16).rearrange("p (c f) -> p c f", c=2)
        for j in range(8):
            Kslot[6 + j] = wDj[:, j, :, :].rearrange("p k f -> p (k f)").rearrange("p (c f) -> p c f", c=2)

        def load_K(b, extra=()):
            srcK = D['ck'][b].rearrange("(c m) f -> m c f", c=2)
            P.dma('gpsimd', d_K[b], lambda e: e.dma_start(out=Kslot[b], in_=srcK), [], [r_K[b]] + list(extra))

        for b in range(6):
            load_K(b)
        sgm = A.bf16(4 * 2048).rearrange("p (h t) -> p h t", h=4)
        pT_ = A.bf16(4096).rearrange("p (n t) -> p n t", n=8)
        r_qm = [[Res() for _ in range(4)] for _ in range(4)]
        r_sgm = [[Res() for _ in range(4)] for _ in range(4)]
        r_pT = [Res() for _ in range(8)]
        rs_ = [A.f32(512) for _ in range(2)]
        mT = [A.f32(512) for _ in range(2)]
        r_rs, r_mT = [Res(), Res()], [Res(), Res()]
        nb = [0]

        def nbank():
            nb[0] = (nb[0] + 1) % 8
            return nb[0]

        for j in range(8):
            for tb in range(4):
                c0 = tb * 512
                bk = nbank()
                for kc in range(8):
                    mm(banks[bk], wDj[:, j, kc, :], hT[:, kc, c0:c0 + 512], kc == 0, kc == 7,
                       [r_hT[tb * 4 + i] for i in range(4)] + [r_wDj[j]], [r_bank[bk]], kc == 7)
                if j < 4:
                    op('scalar', lambda e, j=j, bk=bk, c0=c0: e.activation(out=qmT[:, j, c0:c0 + 512], in_=banks[bk], func=AF.Copy),
                       [r_bank[bk]], [r_qm[j][tb]])
                else:
                    op('scalar', lambda e, j=j, bk=bk, c0=c0: e.activation(out=sgm[:, j - 4, c0:c0 + 512], in_=banks[bk], func=AF.Silu),
                       [r_bank[bk]], [r_sgm[j - 4][tb]])
        for j in range(2):
            bk = nbank()
            for jj in range(4):
                for kc in range(8):
                    mm(banks[bk][0:16, jj * 128:(jj + 1) * 128], hT[:, kc, 2048:2064], wDj[:, j * 4 + jj, kc, :], kc == 0, kc == 7,
                       [r_hT[16], r_wDj[j * 4 + jj]], [r_bank[bk]], kc == 7)
            if j == 0:
                op('scalar', lambda e, bk=bk: e.copy(out=qm_s[0:16], in_=banks[bk][0:16]), [r_bank[bk]], [r_qms])
            else:
                op('scalar', lambda e, bk=bk: e.activation(out=sgm_s[0:16], in_=banks[bk][0:16], func=AF.Silu), [r_bank[bk]], [r_sgms])
        for j in range(8):
            load_K(6 + j, extra=[r_wDj[j]])
        for tb in range(4):
            c0 = tb * 512
            for h in range(4):
                for mc in range(2):
                    bk = nbank()
                    mm(banks[bk], kT[:, h, mc * 128:(mc + 1) * 128], qmT[:, h, c0:c0 + 512], True, True, [r_kv, r_qm[h][tb]], [r_bank[bk]], True)
                    op('scalar', lambda e, h=h, mc=mc, bk=bk: e.activation(out=pT_[:, mc * 4 + h, :], in_=banks[bk], func=AF.Exp),
                       [r_bank[bk]], [r_pT[mc * 4 + h]])
            for h in range(4):
                i2 = h % 2
                bko, bkd = nbank(), nbank()
                for mc in range(2):
                    mm(banks[bko], vmem[:, mc, h * 128:(h + 1) * 128], pT_[:, mc * 4 + h, :], mc == 0, mc == 1, [r_kv, r_pT[mc * 4 + h]], [r_bank[bko]], mc == 1)
                for mc in range(2):
                    mm(banks[bkd], ones_bf, pT_[:, mc * 4 + h, :], mc == 0, mc == 1, [r_const, r_pT[mc * 4 + h]], [r_bank[bkd]], mc == 1)
                op('scalar', lambda e, i2=i2, bkd=bkd: e.activation(out=rs_[i2], in_=banks[bkd], func=AF.Ln), [r_bank[bkd]], [r_rs[i2]])
                op('scalar', lambda e, i2=i2: e.activation(out=rs_[i2], in_=rs_[i2], func=AF.Exp, scale=-1.0), [r_rs[i2]], [r_rs[i2]])
                op('vector', lambda e, i2=i2, bko=bko: e.tensor_tensor(out=mT[i2], in0=banks[bko], in1=rs_[i2], op=ALU.mult),
                   [r_bank[bko], r_rs[i2]], [r_mT[i2]])
                op('vector', lambda e, i2=i2, h=h, c0=c0: e.tensor_tensor(out=preT[2][:, h, c0:c0 + 512], in0=mT[i2], in1=sgm[:, h, c0:c0 + 512], op=ALU.mult),
                   [r_mT[i2], r_sgm[h][tb]], [r_preT[2][tb]])

        P.barrier()
        A.off = D2_LOCAL
        delta16 = A.f32(256).rearrange("p (a b) -> p a b", a=16)
        r_d2 = Res()
        load('sync', delta16, D['delta16'], r_d2)
        loads_done()
        pm_s = A.f32(512)
        r_pms = Res()
        qmTs = A.f32(64).rearrange("p (h b) -> p h b", h=4)
        QmM = A.bf16(1024).rearrange("p (h b c) -> p h b c", h=4, b=16)
        r_qmTs, r_QmM = Res(), Res()
        NV = 6
        for i in (14, 15):
            Kslot[i] = A.bf16(1024).rearrange("p (c f) -> p c f", c=2)
        Vb = [A.bf16(1024).rearrange("p (c f) -> p c f", c=2) for _ in range(NV)]
        r_Vb = [Res() for _ in range(NV)]
        d_Vb = [P.dma_sem() for _ in range(NV)]
        KbT = [A.bf16(1024).rearrange("p (h m) -> p h m", h=4) for _ in range(2)]
        r_KbT = [Res(), Res()]
        p_s = A.f32(1024).rearrange("p (h m) -> p h m", h=4)
        r_ps = Res()
        den, rden = A.f32(4), A.f32(4)
        r_den = Res()
        pTs = A.f32(128).rearrange("p (n b) -> p n b", n=8)
        PTm = A.bf16(2048).rearrange("p (n b c) -> p n b c", n=8, b=16)
        r_pTs, r_PTm = Res(), Res()

        def vslot(b):
            if b < NV:
                return Vb[b], r_Vb[b], d_Vb[b]
            return Kslot[b - NV], r_K[b - NV], d_K[b - NV]

        def load_V(b):
            dst, res, dsm = vslot(b)
            srcV = D['cv'][b].rearrange("(c m) f -> m c f", c=2)
            P.dma('gpsimd', dsm, lambda e: e.dma_start(out=dst, in_=srcV), [], [res])

        load_K(14)
        load_K(15)
        for b in range(NV):
            load_V(b)
        TB = [0, 1]
        SB = [2, 3, 4, 5]
        XB = 6
        for h in range(4):
            tr(banks[XB][:, h * 16:(h + 1) * 16], qm_s[0:16, h * 128:(h + 1) * 128], identf[0:16, 0:16], [r_qms, r_const], [r_bank[XB]], h == 3)
        op('vector', lambda e: e.tensor_copy(out=qmTs, in_=banks[XB][:, 0:64].rearrange("p (h b) -> p h b", h=4)), [r_bank[XB]], [r_qmTs])
        op('vector', lambda e: e.tensor_tensor(out=QmM, in0=qmTs.unsqueeze(3).to_broadcast([128, 4, 16, 16]),
                                               in1=delta16.unsqueeze(1).to_broadcast([128, 4, 16, 16]), op=ALU.mult),
           [r_qmTs, r_d2], [r_QmM])

        def trK(b):
            s2 = b % 2
            tb_ = TB[s2]
            for h in range(4):
                for mc in range(2):
                    tr(bankb[tb_][:, h * 256 + mc * 128:h * 256 + (mc + 1) * 128], Kslot[b][:, mc, h * 128:(h + 1) * 128], identb,
                       [r_K[b], r_const], [r_bank[tb_]], h == 3 and mc == 1)
            op('scalar', lambda e: e.copy(out=KbT[s2].rearrange("p h m -> p (h m)"), in_=bankb[tb_]), [r_bank[tb_]], [r_KbT[s2]])
            if b + NV < 16:
                load_V(b + NV)

        trK(0)
        for b in range(16):
            s2 = b % 2
            if b + 1 < 16:
                trK(b + 1)
            for h in range(4):
                mm(banks[SB[h]][0:16, 0:256], QmM[:, h, b, :], KbT[s2][:, h, :], b == 0, b == 15, [r_QmM, r_KbT[s2]], [r_bank[SB[h]]], True)
        for h in range(4):
            op('scalar', lambda e, h=h: e.activation(out=p_s[0:16, h, :], in_=banks[SB[h]][0:16, 0:256], func=AF.Exp, scale=128.0 ** -0.5,
                                                     accum_out=den[0:16, h:h + 1]), [r_bank[SB[h]]], [r_ps])
        op('vector', lambda e: e.reciprocal(out=rden[0:16], in_=den[0:16]), [r_ps], [r_den])
        for h in range(4):
            for mc in range(2):
                tr(banks[XB][:, 64 + (mc * 4 + h) * 16:64 + (mc * 4 + h + 1) * 16], p_s[0:16, h, mc * 128:(mc + 1) * 128], identf[0:16, 0:16],
                   [r_ps, r_const], [r_bank[XB]], h == 3 and mc == 1)
        op('vector', lambda e: e.tensor_copy(out=pTs, in_=banks[XB][:, 64:192].rearrange("p (n b) -> p n b", n=8)), [r_bank[XB]], [r_pTs])
        op('vector', lambda e: e.tensor_tensor(out=PTm, in0=pTs.unsqueeze(3).to_broadcast([128, 8, 16, 16]),
                                               in1=delta16.unsqueeze(1).to_broadcast([128, 8, 16, 16]), op=ALU.mult), [r_pTs, r_d2], [r_PTm])
        for b in range(16):
            vs, vres, _ = vslot(b)
            for h in range(4):
                for mc in range(2):
                    mm(banks[SB[h]][0:16, 256:384], PTm[:, mc * 4 + h, b, :], vs[:, mc, h * 128:(h + 1) * 128],
                       b == 0 and mc == 0, b == 15 and mc == 1, [r_PTm, vres], [r_bank[SB[h]]], True)
        for h in range(4):
            sl = slice(h * 128, (h + 1) * 128)
            op('vector', lambda e, h=h, sl=sl: e.tensor_scalar(out=pm_s[0:16, sl], in0=banks[SB[h]][0:16, 256:384], scalar1=rden[0:16, h:h + 1], scalar2=None,
                                                             op0=ALU.mult), [r_bank[SB[h]], r_den], [r_pms])
        op('vector', lambda e: e.tensor_tensor(out=pm_s[0:16], in0=pm_s[0:16], in1=sgm_s[0:16], op=ALU.mult), [r_pms, r_sgms], [r_pms])
        bk = 7
        for h in range(4):
            tr(banks[bk][:, h * 16:(h + 1) * 16], pm_s[0:16, h * 128:(h + 1) * 128], identf[0:16, 0:16], [r_pms, r_const], [r_bank[bk]], h == 3)
        op('scalar', lambda e, bk=bk: e.copy(out=preT[2][:, :, 2048:2064], in_=banks[bk][:, 0:64].rearrange("p (h b) -> p h b", h=4)),
           [r_bank[bk]], [r_preT[2][4]])

    def phase_E(wOv, r_wO, EB):
        mergedT = A.bf16(8 * NTOK).rearrange("p (k t) -> p k t", k=8)
        r_mg = [Res() for _ in range(5)]
        xst = [A.f32(1024) for _ in range(4)]
        r_xst = [Res() for _ in range(4)]
        d_xst = [P.dma_sem() for _ in range(4)]
        fnwb = A.f32(1024)
        ss, rstd = A.f32(20), A.f32(20)
        r_ss = [Res() for _ in range(20)]
        r_fn = Res()
        load('sync', fnwb, D['final_norm_w'].partition_broadcast(128), r_fn)
        loads_done()
        wg, wbr, r_wg, r_wbr, tbuf, junk, load_fc = EB
        r_junkE = Res()
        r_t = [[Res() for _ in range(3)] for _ in range(2)]
        nb = [0]

        def nbank():
            nb[0] = (nb[0] + 1) % 8
            return nb[0]

        tiles = [(D['xp'][t * 128:(t + 1) * 128, :], D['yp'][t * 128:(t + 1) * 128, :], 128, t * 128, t // 4) for t in range(16)]
        tiles.append((D['xs'], D['ys'], 16, 2048, 4))

        def ldx(i):
            src, dst, nr, c0, bi = tiles[i]
            s = i % 4
            xt = xst[s][0:nr]
            P.dma('sync', d_xst[s], lambda e: e.dma_start(out=xt, in_=src), [], [r_xst[s]])

        for i in range(4):
            ldx(i)
        for kc in range(4):
            op('vector', lambda e, kc=kc: e.tensor_scalar(out=preT[0][:, kc, 0:2048], in0=preT[0][:, kc, 0:2048], scalar1=gnwT[:, kc:kc + 1],
                                                         scalar2=None, op0=ALU.mult), [r_const] + r_preT[0][0:4], r_preT[0][0:4])
        blocks = [(tb * 512, 512, tb) for tb in range(4)] + [(2048, 16, 4)]
        it = 0
        for fc in range(8):
            s = fc % 2
            if fc + 1 < 8:
                load_fc(fc + 1)
            for (c0, n, bi) in blocks:
                hres = [r_hT[16]] if bi == 4 else [r_hT[bi * 4 + i] for i in range(4)]
                ts_ = it % 2
                it += 1
                for g in range(3):
                    bkg = nbank()
                    for kc in range(8):
                        mm(banks[bkg][:, 0:n], wg[s][:, g, kc, :], hT[:, kc, c0:c0 + n], kc == 0, kc == 7, hres + [r_wg[s]], [r_bank[bkg]], kc == 7)
                    tg = tbuf[ts_][g][:, 0:n]
                    op('scalar', lambda e, tg=tg, bkg=bkg, n=n: e.activation(out=tg, in_=banks[bkg][:, 0:n], func=AF.Tanh, scale=0.5),
                       [r_bank[bkg]], [r_t[ts_][g]])
                    bkb = nbank()
                    for kc in range(4):
                        mm(banks[bkb][:, 0:n], wbr[s][:, g, kc, :], preT[g][:, kc, c0:c0 + n], kc == 0, kc == 3, [r_preT[g][bi], r_wbr[s]], [r_bank[bkb]], kc == 3)
                    op('vector', lambda e, tg=tg, bkb=bkb, n=n: e.scalar_tensor_tensor(out=tg, in0=tg, scalar=1.0, in1=banks[bkb][:, 0:n],
                                                                                        op0=ALU.add, op1=ALU.mult), [r_t[ts_][g], r_bank[bkb]], [r_t[ts_][g]])
                t0, t1, t2 = [tbuf[ts_][g][:, 0:n] for g in range(3)]
                op('gpsimd', lambda e, t0=t0, t1=t1: e.tensor_tensor(out=t0, in0=t0, in1=t1, op=ALU.add), [r_t[ts_][0], r_t[ts_][1]], [r_t[ts_][0]])
                op('gpsimd', lambda e, t0=t0, t2=t2, fc=fc, c0=c0, n=n: e.tensor_tensor(out=mergedT[:, fc, c0:c0 + n], in0=t0, in1=t2, op=ALU.add),
                   [r_t[ts_][0], r_t[ts_][2]], [r_mg[bi]])
        def e1(i):
            src, dst, nr, c0, bi = tiles[i]
            s = i % 4
            xt = xst[s][0:nr]
            for half in range(2):
                bk = nbank()
                for kc in range(8):
                    mm(banks[bk][0:nr, :], mergedT[:, kc, c0:c0 + nr], wOv[:, kc, half * 512:(half + 1) * 512], kc == 0, kc == 7,
                       [r_mg[bi], r_wO], [r_bank[bk]], kc == 7)
                xh = xt[:, half * 512:(half + 1) * 512]
                op('vector', lambda e, xh=xh, bk=bk: e.scalar_tensor_tensor(out=xh, in0=banks[bk][0:nr, :], scalar=0.5, in1=xh,
                                                                          op0=ALU.mult, op1=ALU.add), [r_bank[bk], r_xst[s]], [r_xst[s]])
            ssi, rsi = ss[0:nr, i:i + 1], rstd[0:nr, i:i + 1]
            op('scalar', lambda e: e.activation(out=junk[0:nr], in_=xt, func=AF.Square, accum_out=ssi), [r_xst[s]], [r_ss[i], r_junkE])
            op('gpsimd', lambda e: e.tensor_scalar(out=rsi, in0=ssi, scalar1=1.0 / 1024, scalar2=EPS, op0=ALU.mult, op1=ALU.add),
               [r_ss[i]], [r_ss[i]])
            rsqrt_inplace(rsi, r_ss[i])

        def e2(i):
            src, dst, nr, c0, bi = tiles[i]
            s = i % 4
            xt = xst[s][0:nr]
            rsi = rstd[0:nr, i:i + 1]
            op('vector', lambda e: e.scalar_tensor_tensor(out=xt, in0=xt, scalar=rsi, in1=fnwb[0:nr], op0=ALU.mult, op1=ALU.mult),
               [r_xst[s], r_ss[i], r_fn], [r_xst[s]])
            store('gpsimd', dst, xt, [r_xst[s]], d_xst[s])

        n = len(tiles)
        e1(0)
        for i in range(n):
            if i + 1 < n:
                e1(i + 1)
            e2(i)
            if i + 4 < n:
                ldx(i + 4)

    B_LOCAL = C_LOCAL = D_LOCAL = PERS
    WA.off = 0
    wBv = WA.bf16(8 * 2048).rearrange("p (k n) -> p k n", k=8)
    r_wB = Res()
    WB.off = 0
    wkv = WB.f32(4096).bitcast(BF16).rearrange("p (k n) -> p k n", k=8)
    r_wkv = Res()
    load_wkv = lambda: load_w(wkv, D['w_mem_kv'].rearrange("(k p) n -> p k n", p=128), r_wkv, nsplit=2)
    def load_wB():
        g = LoadGroup()
        for blk in (3, 2, 1, 0):
            sl = slice(blk * 512, (blk + 1) * 512)
            g.add('gpsimd', wBv[:, :, sl], w_in_v[:, :, sl], r_wB)
        g.done()

    phase_A(wkv, r_wkv, load_wB, load_wkv)
    P.barrier()
    A.off = PERS
    WB.off = 0
    wCv = WB.bf16(8 * 1536).rearrange("p (k n) -> p k n", k=8)
    r_wC = Res()
    cwT = WB.f32(124).rearrange("p (c k) -> p c k", c=4)
    cwTh = WB.f32(124).rearrange("p (c k) -> p c k", c=4)
    cbT, lnwT, lnbT = WB.f32(4), WB.f32(4), WB.f32(4)
    cosS_p = WB.f32(64)
    sinS_p = WB.f32(128).rearrange("p (a f) -> p a f", a=2)
    r_rotS = Res()
    diag = Arena(WA.ap, 8192).bf16(124 * 128).rearrange("p (n c) -> p n c", n=124)
    CV = (diag, [Res() for _ in range(124)], cwT, cwTh, cbT, lnwT, lnbT, Res())
    phase_B(wBv, r_wB, CV, lambda: load_w(wCv, w_in_v[:, :, 2048:3584], r_wC, nsplit=4))
    P.barrier()
    A.off = PERS
    WA.off = 0
    WDJ_OFF = PERS + 1024
    _tmpA = Arena(arena_ap, NW)
    _tmpA.off = WDJ_OFF
    wDj = _tmpA.bf16(8 * 8 * 128).rearrange("p (j k f) -> p j k f", j=8, k=8)
    r_wDj = [Res() for _ in range(8)]

    def load_wDj():
        P.barrier(engines=('gpsimd',))
        for j in range(8):
            c0 = 3584 + j * 128
            dsm = P.dma_sem()
            P.dma('gpsimd', dsm, lambda e, j=j, c0=c0: e.dma_start(out=wDj[:, j, :, :], in_=w_in_v[:, :, c0:c0 + 128]), [], [r_wDj[j]])

    phase_C(wCv, r_wC, CV, WB, WA)
    P.barrier()
    A.off = PERS
    WA.off = 0
    WB.off = 0
    qmT_wa = WA.bf16(4 * 2048).rearrange("p (h t) -> p h t", h=4)
    wOv = WA.bf16(8 * 1024).rearrange("p (k n) -> p k n", k=8)
    r_wO = Res()
    wg = [WB.bf16(3 * 8 * 128).rearrange("p (g k f) -> p g k f", g=3, k=8) for _ in range(2)]
    wbr = [WB.bf16(3 * 4 * 128).rearrange("p (g k f) -> p g k f", g=3, k=4) for _ in range(2)]
    r_wg, r_wbr = [Res(), Res()], [Res(), Res()]
    tbuf = [[WB.f32(512) for _ in range(3)] for _ in range(2)]
    junkE = WB.bf16(1024)
    brw = [D['w_br_ret'], D['w_br_conv'], D['w_br_mem']]
    d_wg, d_wbr = [P.dma_sem(), P.dma_sem()], [P.dma_sem(), P.dma_sem()]

    def load_fc(fc):
        s_ = fc % 2
        for g in range(3):
            c0 = 4608 + g * 1024 + fc * 128
            P.dma('gpsimd', d_wg[s_], lambda e, s_=s_, g=g, c0=c0: e.dma_start(out=wg[s_][:, g, :, :], in_=w_in_v[:, :, c0:c0 + 128]), [], [r_wg[s_]])
        r_wg[s_].w = (id(d_wg[s_].sem), d_wg[s_].count)
        for g in range(3):
            src = brw[g].rearrange("(k p) n -> p k n", p=128)[:, :, fc * 128:(fc + 1) * 128]
            P.dma('gpsimd', d_wbr[s_], lambda e, s_=s_, g=g, src=src: e.dma_start(out=wbr[s_][:, g, :, :], in_=src), [], [r_wbr[s_]])
        r_wbr[s_].w = (id(d_wbr[s_].sem), d_wbr[s_].count)

    load_w(wOv, D['w_out'].rearrange("(k p) n -> p k n", p=128), r_wO, nsplit=2)
    load_fc(0)
    phase_D(wDj, r_wDj, qmT_wa)
    P.barrier()
    A.off = PERS
    phase_E(wOv, r_wO, (wg, wbr, r_wg, r_wbr, tbuf, junkE, load_fc))
    P.barrier(engines=['sync'])
    P.replay()
    return nc


_CACHE = {}


def kernel(x_prompt, x_sample, mem_prompt, state_ret, state_conv, cache_mem_k, cache_mem_v,
           norm_w, w_in, ret_gn_w, conv_w, conv_b, conv_ln_w, conv_ln_b, mem_norm_w,
           w_mem_kv, w_br_ret, w_br_conv, w_br_mem, w_out, final_norm_w):
    f = lambda a: np.ascontiguousarray(np.asarray(a, dtype=np.float32))
    x_prompt, x_sample, mem_prompt = f(x_prompt), f(x_sample), f(mem_prompt)
    state_ret, state_conv, cache_mem_k, cache_mem_v = f(state_ret), f(state_conv), f(cache_mem_k), f(cache_mem_v)
    if 'nc' not in _CACHE:
        _CACHE['nc'] = build_program()
        _CACHE['consts'] = make_consts()
    nc, consts = _CACHE['nc'], _CACHE['consts']

    def colT(v):
        return np.ascontiguousarray(f(v).reshape(4, 128).T)

    cw = f(conv_w)[0]
    shared = {
        'norm_w': f(norm_w)[0], 'w_in': f(w_in)[0], 'gnwT': colT(f(ret_gn_w)[0]),
        'cwT': np.ascontiguousarray(cw.T.reshape(4, 128, 31).transpose(1, 0, 2)),
        'wrep': np.ascontiguousarray(np.tile(cw[0:30], (4, 1))),
        'w30b': np.ascontiguousarray(np.broadcast_to(cw[30][None], (16, 512))),
        'cbb': np.ascontiguousarray(np.broadcast_to(f(conv_b)[0][None], (16, 512))),
        'cbT': colT(f(conv_b)[0]), 'lnwT': colT(f(conv_ln_w)[0]), 'lnbT': colT(f(conv_ln_b)[0]),
        'lnwb': np.ascontiguousarray(np.broadcast_to(f(conv_ln_w)[0][None], (16, 512))),
        'lnbb': np.ascontiguousarray(np.broadcast_to(f(conv_ln_b)[0][None], (16, 512))),
        'mem_norm_w': f(mem_norm_w)[0], 'w_mem_kv': f(w_mem_kv)[0],
        'w_br_ret': f(w_br_ret)[0], 'w_br_conv': f(w_br_conv)[0], 'w_br_mem': f(w_br_mem)[0],
        'w_out': f(w_out)[0], 'final_norm_w': f(final_norm_w),
    }
    for k, v in consts.items():
        shared['c_' + k] = v
    in_maps = []
    for i in range(8):
        m = dict(shared)
        sl = slice(16 * i, 16 * i + 16)
        m['xp'] = x_prompt[i]
        m['xs'] = np.ascontiguousarray(x_sample[sl, 0, :])
        m['memp'] = mem_prompt[i]
        m['sret'] = state_ret[0, sl]
        m['sconv'] = state_conv[0, sl]
        m['ck'] = np.ascontiguousarray(cache_mem_k[0, sl].reshape(16, 256, 512))
        m['cv'] = np.ascontiguousarray(cache_mem_v[0, sl].reshape(16, 256, 512))
        in_maps.append(m)
    res = run_bass_kernel_spmd(nc, in_maps, core_ids=list(range(8)))
    R = res.results
    y_prompt = np.stack([R[i]['yp'] for i in range(8)], axis=0)
    y_sample = np.concatenate([R[i]['ys'] for i in range(8)], axis=0)[:, None, :]
    srp = np.stack([R[i]['srp'] for i in range(8)], axis=0)[None]
    srs = np.concatenate([R[i]['srs'] for i in range(8)], axis=0)[None]
    scp = np.stack([R[i]['scp'] for i in range(8)], axis=0)[None]
    scs = np.concatenate([R[i]['scs'] for i in range(8)], axis=0)[None]
    mkp = np.stack([R[i]['mkp'].reshape(256, 4, 128) for i in range(8)], axis=0)[None]
    mvp = np.stack([R[i]['mvp'].reshape(256, 4, 128) for i in range(8)], axis=0)[None]
    return (y_prompt.astype(np.float32), y_sample.astype(np.float32), srp.astype(np.float32), srs.astype(np.float32),
            scp.astype(np.float32), scs.astype(np.float32), mkp.astype(np.float32), mvp.astype(np.float32))
```
